# Optimizing a Trainium2 kernel written in Bass

```python
import jax, jax.numpy as jnp
from jax import lax
import numpy as np

D_MODEL = 1024
BATCH = 16
SEQ = 256
DEPTH = 4
DEC_BATCH = 8
DEC_SEQ = 2048
PAST_LEN = 512

GRID_W = 64
N_EVEN = (DEPTH + 1) // 2
N_ODD = DEPTH // 2
A_WIDTH = D_MODEL // 2
HEAD_DIM = 64
N_HEADS_A = A_WIDTH // HEAD_DIM
WIN_H_MAX = 8
WIN_W = 16
B_WIDTH = D_MODEL - A_WIDTH
POOL_WINDOWS = (2, 4, 8, 16)
N_POOL = len(POOL_WINDOWS)
POOL_GROUP = B_WIDTH // N_POOL
C_WIDTH = D_MODEL
CONV_K = 31
D_FF = 2816
FFN_CONV_K = 3
N_MOD = 6
Q_BLOCK = 128
EPS = 1e-6
NEG_INF = -1e30

kernel_name = "hybrid_natten_pool_conformer_dit_step"


def rmsnorm(x, g):
    xf = x.astype(jnp.float32)
    y = xf * lax.rsqrt(jnp.mean(xf * xf, axis=-1, keepdims=True) + EPS)
    return (y * g.astype(jnp.float32)).astype(x.dtype)


def layernorm(x, g, b):
    xf = x.astype(jnp.float32)
    mu = jnp.mean(xf, axis=-1, keepdims=True)
    var = jnp.mean(jnp.square(xf - mu), axis=-1, keepdims=True)
    y = (xf - mu) * lax.rsqrt(var + EPS)
    return (y * g.astype(jnp.float32) + b.astype(jnp.float32)).astype(x.dtype)


def dwconv(x, w, b):
    k = w.shape[0]
    y = lax.conv_general_dilated(
        x, w[:, None, :].astype(x.dtype), window_strides=(1,),
        padding=[(k // 2, k // 2)], dimension_numbers=("NWC", "WIO", "NWC"),
        feature_group_count=x.shape[-1])
    return y + b.astype(x.dtype)


def multiscale_pool(p, pool_w, pool_scale):
    s = p.shape[1]
    pf = p.astype(jnp.float32)
    csum = jnp.concatenate([jnp.zeros_like(pf[:, :1]), jnp.cumsum(pf, axis=1)], axis=1)
    t = jnp.arange(s)
    outs = []
    for g, w in enumerate(POOL_WINDOWS):
        lo = jnp.clip(t - w // 2, 0, s - 1)
        hi = jnp.clip(t - w // 2 + w - 1, 0, s - 1)
        sl = slice(g * POOL_GROUP, (g + 1) * POOL_GROUP)
        cg = csum[..., sl]
        cnt = (hi - lo + 1).astype(jnp.float32)[None, :, None]
        outs.append((cg[:, hi + 1] - cg[:, lo]) / cnt - pf[..., sl])
    d = jnp.stack(outs, axis=2)
    y = jnp.einsum("bsgc,gcd->bsgd", d, pool_w.astype(jnp.float32))
    y = y.reshape(p.shape) * pool_scale.astype(jnp.float32)
    return y.astype(p.dtype)


def context_attention(q, k, v):
    b, l, h, dh = q.shape
    nb = l // Q_BLOCK
    qb = q.reshape(b, nb, Q_BLOCK, h, dh).transpose(1, 0, 2, 3, 4)
    scale = dh ** -0.5

    def blk(qi):
        s = jnp.einsum("bqhd,bkhd->bhqk", qi, k).astype(jnp.float32) * scale
        pr = jax.nn.softmax(s, axis=-1).astype(v.dtype)
        return jnp.einsum("bhqk,bkhd->bqhd", pr, v)

    o = lax.map(blk, qb)
    return o.transpose(1, 0, 2, 3, 4).reshape(b, l, h, dh)


def neighbourhood_attention(q, k, v, ck, cv, rpb):
    b, s, h, dh = q.shape
    rows = s // GRID_W
    kh = min(WIN_H_MAX, rows)
    kw = min(WIN_W, GRID_W)
    scale = dh ** -0.5
    r = jnp.arange(rows)
    c = jnp.arange(GRID_W)
    row_start = jnp.clip(r - kh // 2, 0, rows - kh)
    col_start = jnp.clip(c - kw // 2, 0, GRID_W - kw)
    col_valid = (c[None, :] >= col_start[:, None]) & (c[None, :] < col_start[:, None] + kw)
    drow = row_start[:, None] + jnp.arange(kh)[None, :] - r[:, None] + (WIN_H_MAX - 1)
    dcol = jnp.clip(c[None, :] - c[:, None], -(kw - 1), kw - 1) + (WIN_W - 1)
    bias = rpb.astype(jnp.float32)[:, drow[:, None, :, None], dcol[None, :, None, :]]
    bias = jnp.where(col_valid[None, None, :, None, :], bias, NEG_INF)
    bias = bias.transpose(1, 0, 2, 3, 4).reshape(rows, h, GRID_W, kh * GRID_W)
    kg = k.reshape(b, rows, GRID_W, h, dh)
    vg = v.reshape(b, rows, GRID_W, h, dh)
    qr = q.reshape(b, rows, GRID_W, h, dh).transpose(1, 0, 2, 3, 4)
    n_local = kh * GRID_W

    def row(args):
        qi, rs, bi = args
        kb = lax.dynamic_slice_in_dim(kg, rs, kh, axis=1).reshape(b, n_local, h, dh)
        vb = lax.dynamic_slice_in_dim(vg, rs, kh, axis=1).reshape(b, n_local, h, dh)
        s_loc = jnp.einsum("bqhd,bkhd->bhqk", qi, kb).astype(jnp.float32) * scale + bi[None]
        s_ctx = jnp.einsum("bqhd,bkhd->bhqk", qi, ck).astype(jnp.float32) * scale
        pr = jax.nn.softmax(jnp.concatenate([s_loc, s_ctx], axis=-1), axis=-1).astype(v.dtype)
        return (jnp.einsum("bhqk,bkhd->bqhd", pr[..., :n_local], vb)
                + jnp.einsum("bhqk,bkhd->bqhd", pr[..., n_local:], cv))

    o = lax.map(row, (qr, row_start, bias))
    return o.transpose(1, 0, 2, 3, 4).reshape(b, s, h, dh)


def even_project(h, w_in, q_gain, k_gain):
    b, s, _ = h.shape
    proj = h @ w_in
    q = rmsnorm(proj[..., :A_WIDTH].reshape(b, s, N_HEADS_A, HEAD_DIM), q_gain)
    k = rmsnorm(proj[..., A_WIDTH:2 * A_WIDTH].reshape(b, s, N_HEADS_A, HEAD_DIM), k_gain)
    v = proj[..., 2 * A_WIDTH:3 * A_WIDTH].reshape(b, s, N_HEADS_A, HEAD_DIM)
    p = proj[..., 3 * A_WIDTH:]
    return q, k, v, p


def even_output(attn, p, pool_w, pool_scale, w_out):
    b, s = p.shape[:2]
    y = jnp.concatenate([attn.reshape(b, s, A_WIDTH), multiscale_pool(p, pool_w, pool_scale)], axis=-1)
    return y @ w_out


def conformer_conv(h, w_pw1, dw_w, dw_b, ln_g, ln_b, w_pw2):
    u = h @ w_pw1
    a, g = jnp.split(u, 2, axis=-1)
    u = a * jax.nn.sigmoid(g)
    u = dwconv(u, dw_w, dw_b)
    u = jax.nn.silu(layernorm(u, ln_g, ln_b))
    return u @ w_pw2


def conv_ffn(h, w_in, conv_w, conv_b, w_out):
    u = dwconv(h @ w_in, conv_w, conv_b)
    a, g = jnp.split(u, 2, axis=-1)
    return (jax.nn.silu(a) * g) @ w_out


def setup_inputs(seed: int = 0) -> dict:
    key = jax.random.key(seed)
    ks = jax.random.split(key, 32)
    f32 = jnp.float32

    def nrm(k, shape, s):
        return jax.random.normal(k, shape, f32) * s

    def gain(k, shape):
        return 1.0 + 0.02 * jax.random.normal(k, shape, f32)

    D = D_MODEL
    return {
        "x_prompt": nrm(ks[0], (BATCH, SEQ, D), 1.0),
        "x_sample": nrm(ks[1], (DEC_BATCH, DEC_SEQ, D), 1.0),
        "cache_k": nrm(ks[2], (DEC_BATCH, N_EVEN, PAST_LEN, N_HEADS_A, HEAD_DIM), 1.0),
        "cache_v": nrm(ks[3], (DEC_BATCH, N_EVEN, PAST_LEN, N_HEADS_A, HEAD_DIM), 1.0),
        "c": nrm(ks[4], (DEC_BATCH, D), 1.0),
        "c_ctx": nrm(ks[5], (D,), 1.0),
        "norm_mix": gain(ks[6], (DEPTH, D)),
        "norm_ffn": gain(ks[7], (DEPTH, D)),
        "w_mod": nrm(ks[8], (DEPTH, D, N_MOD * D), 0.5 * D ** -0.5),
        "b_mod": nrm(ks[9], (DEPTH, N_MOD * D), 0.02),
        "w_in_ab": nrm(ks[10], (N_EVEN, D, 3 * A_WIDTH + B_WIDTH), D ** -0.5),
        "q_gain": gain(ks[11], (N_EVEN, HEAD_DIM)),
        "k_gain": gain(ks[12], (N_EVEN, HEAD_DIM)),
        "rpb": nrm(ks[13], (N_EVEN, N_HEADS_A, 2 * WIN_H_MAX - 1, 2 * WIN_W - 1), 0.1),
        "pool_w": nrm(ks[14], (N_EVEN, N_POOL, POOL_GROUP, POOL_GROUP), POOL_GROUP ** -0.5),
        "pool_scale": gain(ks[15], (N_EVEN, B_WIDTH)),
        "w_out_ab": nrm(ks[16], (N_EVEN, A_WIDTH + B_WIDTH, D), (A_WIDTH + B_WIDTH) ** -0.5),
        "conv_pw1": nrm(ks[17], (N_ODD, D, 2 * C_WIDTH), D ** -0.5),
        "conv_dw": nrm(ks[18], (N_ODD, CONV_K, C_WIDTH), CONV_K ** -0.5),
        "conv_dw_b": nrm(ks[19], (N_ODD, C_WIDTH), 0.02),
        "conv_ln_g": gain(ks[20], (N_ODD, C_WIDTH)),
        "conv_ln_b": nrm(ks[21], (N_ODD, C_WIDTH), 0.02),
        "conv_pw2": nrm(ks[22], (N_ODD, C_WIDTH, D), C_WIDTH ** -0.5),
        "ffn_w_in": nrm(ks[23], (DEPTH, D, 2 * D_FF), D ** -0.5),
        "ffn_conv_w": nrm(ks[24], (DEPTH, FFN_CONV_K, 2 * D_FF), FFN_CONV_K ** -0.5),
        "ffn_conv_b": nrm(ks[25], (DEPTH, 2 * D_FF), 0.02),
        "ffn_w_out": nrm(ks[26], (DEPTH, D_FF, D), D_FF ** -0.5),
    }


def reference(x_prompt, x_sample, cache_k, cache_v, c, c_ctx, norm_mix, norm_ffn, w_mod, b_mod,
              w_in_ab, q_gain, k_gain, rpb, pool_w, pool_scale, w_out_ab,
              conv_pw1, conv_dw, conv_dw_b, conv_ln_g, conv_ln_b, conv_pw2,
              ffn_w_in, ffn_conv_w, ffn_conv_b, ffn_w_out):
    xc = x_prompt
    xl = x_sample
    sc = jax.nn.silu(c_ctx)[None, None, :]
    sl = jax.nn.silu(c)[:, None, :]
    new_k, new_v = [], []
    for i in range(DEPTH):
        mc = jnp.split(sc @ w_mod[i] + b_mod[i], N_MOD, axis=-1)
        ml = jnp.split(sl @ w_mod[i] + b_mod[i], N_MOD, axis=-1)
        hc = rmsnorm(xc, norm_mix[i]) * (1 + mc[1]) + mc[0]
        hl = rmsnorm(xl, norm_mix[i]) * (1 + ml[1]) + ml[0]
        j = i // 2
        if i % 2 == 0:
            qc, kc, vc, pc = even_project(hc, w_in_ab[j], q_gain[j], k_gain[j])
            ql, kl, vl, pl = even_project(hl, w_in_ab[j], q_gain[j], k_gain[j])
            ac = context_attention(qc, kc, vc)
            al = neighbourhood_attention(ql, kl, vl, cache_k[:, j], cache_v[:, j], rpb[j])
            yc = even_output(ac, pc, pool_w[j], pool_scale[j], w_out_ab[j])
            yl = even_output(al, pl, pool_w[j], pool_scale[j], w_out_ab[j])
            new_k.append(kc)
            new_v.append(vc)
        else:
            yc = conformer_conv(hc, conv_pw1[j], conv_dw[j], conv_dw_b[j], conv_ln_g[j], conv_ln_b[j], conv_pw2[j])
            yl = conformer_conv(hl, conv_pw1[j], conv_dw[j], conv_dw_b[j], conv_ln_g[j], conv_ln_b[j], conv_pw2[j])
        xc = xc + mc[2] * yc
        xl = xl + ml[2] * yl
        hc = rmsnorm(xc, norm_ffn[i]) * (1 + mc[4]) + mc[3]
        hl = rmsnorm(xl, norm_ffn[i]) * (1 + ml[4]) + ml[3]
        xc = xc + mc[5] * conv_ffn(hc, ffn_w_in[i], ffn_conv_w[i], ffn_conv_b[i], ffn_w_out[i])
        xl = xl + ml[5] * conv_ffn(hl, ffn_w_in[i], ffn_conv_w[i], ffn_conv_b[i], ffn_w_out[i])
    new_cache_k = jnp.stack(new_k, axis=1)
    new_cache_v = jnp.stack(new_v, axis=1)
    return (xc, xl, new_cache_k, new_cache_v)
```

```python
import numpy as np
import concourse.bass as bass
import concourse.mybir as mybir
from concourse.bass_utils import run_bass_kernel_spmd

F32 = mybir.dt.float32
BF16 = mybir.dt.bfloat16
AF = mybir.ActivationFunctionType
ALU = mybir.AluOpType

ENGS = ("pe", "act", "dve", "pool", "sp")
NT = 2560
EPS = 1e-6
TL = [(0, 512, 0), (512, 512, 0), (1024, 512, 0), (1536, 512, 0), (2048, 512, 1)]
NPIECE = 11
DD = 12
NBLK = 30


class Buf:
    __slots__ = ("writer", "readers", "excl")

    def __init__(self, excl=False):
        self.writer = None
        self.readers = []
        self.excl = excl


class DmaSem:
    __slots__ = ("sem", "count")

    def __init__(self, sem):
        self.sem = sem
        self.count = 0


class Op:
    __slots__ = ("eng", "fn", "deps", "dsem", "dval", "signal", "sigval", "is_dma", "rawdeps")


class Sched:
    def __init__(self):
        self.ops = {e: [] for e in ENGS}
        self.dsems = []
        self.bar_ops = []
        self.bar_raw = []

    def new_dsem(self, sem):
        d = DmaSem(sem)
        self.dsems.append(d)
        return d

    def op(self, eng, fn, reads=(), writes=()):
        o = Op()
        o.eng = eng
        o.fn = fn
        o.is_dma = False
        o.dsem = None
        o.dval = 0
        o.signal = False
        o.sigval = 0
        deps = list(self.bar_ops)
        o.rawdeps = self.bar_raw
        for r in reads:
            if r.writer is not None:
                deps.append(r.writer)
            if r.excl:
                deps.extend(x for x in r.readers if x.eng != eng)
        for w in writes:
            if w.writer is not None:
                deps.append(w.writer)
            deps.extend(w.readers)
        for w in writes:
            w.writer = o
            w.readers = []
        for r in reads:
            r.readers.append(o)
        o.deps = deps
        self.ops[eng].append(o)
        return o

    def dma(self, eng, fn, dsem, reads=(), writes=()):
        o = self.op(eng, fn, reads, writes)
        o.is_dma = True
        dsem.count += 16
        o.dsem = dsem
        o.dval = dsem.count
        return o

    def barrier(self):
        ops = []
        for e in ENGS:
            for o in reversed(self.ops[e]):
                if not o.is_dma:
                    ops.append(o)
                    break
        self.bar_ops = ops
        self.bar_raw = [(d, d.count) for d in self.dsems if d.count > 0]

    def emit(self, block, sems, final_waits=()):
        for e in ENGS:
            for o in self.ops[e]:
                for d in o.deps:
                    if d.is_dma:
                        continue
                    if d.eng == o.eng and d.eng in ("pe", "sp"):
                        continue
                    d.signal = True
        for e in ENGS:
            c = 0
            for o in self.ops[e]:
                if o.is_dma:
                    continue
                if o.signal:
                    c += 1
                    o.sigval = c

        def run(e, eng):
            known = {}
            for o in self.ops[e]:
                need = {}
                for d in o.deps:
                    if d.is_dma:
                        key = ("d", id(d.dsem))
                        sem, val = d.dsem.sem, d.dval
                    else:
                        if d.eng == e and e in ("pe", "sp"):
                            continue
                        key = ("c", d.eng)
                        sem, val = sems[d.eng], d.sigval
                    if key not in need or need[key][1] < val:
                        need[key] = (sem, val)
                for (ds, val) in o.rawdeps:
                    key = ("d", id(ds))
                    if key not in need or need[key][1] < val:
                        need[key] = (ds.sem, val)
                for key, (sem, val) in need.items():
                    if known.get(key, 0) >= val:
                        continue
                    eng.wait_ge(sem, val)
                    known[key] = val
                ins = o.fn(eng)
                if o.is_dma:
                    ins.then_inc(o.dsem.sem, 16)
                elif o.signal:
                    ins.then_inc(sems[e], 1)
            if e == "sp":
                for ds in final_waits:
                    if ds.count > 0:
                        eng.wait_ge(ds.sem, ds.count)

        @block.tensor
        def _(eng):
            run("pe", eng)

        @block.scalar
        def _(eng):
            run("act", eng)

        @block.vector
        def _(eng):
            run("dve", eng)

        @block.gpsimd
        def _(eng):
            run("pool", eng)

        @block.sync
        def _(eng):
            run("sp", eng)


def _vec_layout():
    off = {}
    n = 0
    for name, sz in (("cvec", 16), ("bmod", 4 * 48), ("nmix", 32), ("nffn", 32),
                     ("fcw", 4 * 44 * 3), ("fcb", 4 * 44), ("dwb", 16), ("lng", 16),
                     ("lnb", 16), ("pscale", 8), ("qg", 2), ("kg", 2)):
        off[name] = n
        n += sz
    return off, n


VOFF, NV = _vec_layout()


def build_program(stop_after=None):
    nc = bass.Bass("TRN2", target_bir_lowering=False)

    def din(name, shape):
        return nc.dram_tensor(name, list(shape), F32, kind="ExternalInput").ap()

    xT_d = din("xT", [128, 8 * NT])
    vecs_d = din("vecs", [128, NV])
    cmat_d = din("cmat", [128, 4 * 128])
    wmod_d = din("wmod", [4 * 12 * 128, 4096])
    winab_d = din("winab", [2 * 8 * 128, 2048])
    woa_d = din("woa", [2 * 64, 8192])
    wop_d = din("wop", [2 * 128, 4096])
    poolw_d = din("poolw", [2 * 128, 512])
    band_d = din("band", [4 * 4 * 128, 6 * 512])
    bandc_d = din("bandc", [4 * 128, 2 * 256])
    rpbx_d = din("rpbx", [2 * 8 * 64, 960])
    ck_d = din("ck", [2 * 128, 2048])
    cv_d = din("cv", [2 * 128, 2048])
    pw1_d = din("pw1", [2 * 8 * 128, 2048])
    pw2_d = din("pw2", [2 * 128, 8192])
    dwd_d = din("dwd", [2 * 8 * 128, 31 * 128])
    fwi_d = din("fwi", [4 * NPIECE * 128, 4096])
    fwo_d = din("fwo", [4 * NPIECE * 128, 2048])

    yT_d = nc.dram_tensor("yT", [128, 8 * NT], F32, kind="ExternalOutput").ap()
    nk_d = nc.dram_tensor("nk", [2 * 128, 2048], F32, kind="ExternalOutput").ap()
    nv_d = nc.dram_tensor("nv", [2 * 128, 2048], F32, kind="ExternalOutput").ap()
    qs_d = nc.dram_tensor("qscr", [128, 4 * NT], BF16, kind="ExternalOutput").ap()
    tab_d = nc.dram_tensor("tabscr", [2 * 8 * 128, 2 * NBLK * 64], BF16, kind="ExternalOutput").ap()

    ARENA_F32 = 53200
    arena = nc.alloc_sbuf_tensor("arena", [128, ARENA_F32], F32)
    psum = nc.alloc_psum_tensor("psum", [128, 4096], F32)

    S = Sched()
    state = {"top": 0, "bank": 0, "skip": None}

    def alloc(shape, dtype):
        n = 1
        for s in shape:
            n *= s
        nb = n * (2 if dtype == BF16 else 4)
        nb = (nb + 63) // 64 * 64
        sk = state["skip"]
        if sk is not None and state["top"] < sk[1] and state["top"] + nb > sk[0]:
            state["top"] = sk[1]
        o4 = state["top"] // 4
        state["top"] += nb
        assert state["top"] <= ARENA_F32 * 4, ("SBUF arena overflow", state["top"])
        v = arena[:, o4:o4 + nb // 4]
        if dtype == BF16:
            v = v.bitcast(BF16)
        v = v[:, 0:n]
        if len(shape) == 2:
            v = v.rearrange("p (a b) -> p a b", b=shape[1])
        elif len(shape) == 3:
            v = v.rearrange("p (a b c) -> p a b c", b=shape[1], c=shape[2])
        elif len(shape) == 4:
            v = v.rearrange("p (a b c d) -> p a b c d", b=shape[1], c=shape[2], d=shape[3])
        return v

    PB = [Buf(excl=True) for _ in range(8)]

    reserved = set()

    def bank(reserve=False):
        i = state["bank"]
        while i in reserved:
            i = (i + 1) % 8
        state["bank"] = (i + 1) % 8
        if reserve:
            reserved.add(i)
        return psum[:, i * 512:(i + 1) * 512], PB[i]

    gstate = {}

    def gbank(name, ids):
        k = gstate.get(name, 0)
        gstate[name] = k + 1
        i = ids[k % len(ids)]
        return psum[:, i * 512:(i + 1) * 512], PB[i]

    def MM(out, lhsT, rhs, start, stop, reads, writes):
        S.op("pe", lambda e: e.matmul(out, lhsT, rhs, start=start, stop=stop), reads, writes)

    def ACT(out, in_, func, reads, writes, bias=None, scale=None):
        kw = {}
        if bias is not None:
            kw["bias"] = bias
        if scale is not None:
            kw["scale"] = scale
        S.op("act", lambda e: e.activation(out=out, in_=in_, func=func, **kw), reads, writes)

    def TT(eng, out, in0, in1, op, reads, writes):
        S.op(eng, lambda e: e.tensor_tensor(out=out, in0=in0, in1=in1, op=op), reads, writes)

    def STT(eng, out, in0, scalar, in1, op0, op1, reads, writes):
        S.op(eng, lambda e: e.scalar_tensor_tensor(out=out, in0=in0, scalar=scalar, in1=in1,
                                                   op0=op0, op1=op1), reads, writes)

    def RECIP(out, in_, reads, writes):
        S.op("dve", lambda e: e.reciprocal(out=out, in_=in_), reads, writes)

    def MEMSET(eng, ap, val, writes):
        S.op(eng, lambda e: e.memset(ap, val), (), writes)

    def DMA(eng, out, in_, dsem, reads, writes):
        S.dma(eng, lambda e: e.dma_start(out=out, in_=in_), dsem, reads, writes)

    from contextlib import ExitStack
    es = ExitStack()
    with es:
        E = es.enter_context
        sems = {e: E(nc.semaphore("s_" + e)) for e in ENGS}
        dpool = [S.new_dsem(E(nc.semaphore("d%d" % i))) for i in range(40)]
        dstate = {"i": 0}

        def dsem():
            d = dpool[dstate["i"] % len(dpool)]
            dstate["i"] += 1
            return d

        dout = S.new_dsem(E(nc.semaphore("dout")))
        block = E(nc.Block())

        X = alloc([8, NT], F32)
        XB = [[Buf() for _ in range(5)] for _ in range(8)]
        VEC = alloc([NV], F32)
        CM = alloc([4, 128], BF16)
        ONESF = alloc([64], F32)
        MOD = alloc([4, 6, 8, 2], F32)
        AV = alloc([2, 8, 2], F32)
        QG8 = alloc([1], F32)
        bVEC, bCM, bAV, bONESF, bQG8 = Buf(), Buf(), Buf(), Buf(), Buf()
        bMODL = [Buf() for _ in range(4)]
        ST = alloc([8, 2], BF16)
        bST = Buf()
        PERSIST_TOP = state["top"]

        def vec(name, *idx):
            o = VOFF[name]
            dims = {"cvec": (8, 2), "bmod": (4, 48), "nmix": (4, 8), "nffn": (4, 8),
                    "fcw": (4, 44, 3), "fcb": (4, 44), "dwb": (2, 8), "lng": (2, 8),
                    "lnb": (2, 8), "pscale": (2, 4), "qg": (2,), "kg": (2,)}[name]
            lin = 0
            for d, i_ in zip(dims, idx):
                lin = lin * d + i_
            return VEC[:, o + lin:o + lin + 1]

        ONES1024 = CM[:, 0, :]
        BLK64 = CM[:, 1, :]

        d0 = dsem()
        for kc in range(8):
            DMA("sp", X[:, kc, :], xT_d[:, kc * NT:(kc + 1) * NT], d0, (), [XB[kc][t] for t in range(5)])
        d1 = dsem()
        DMA("sp", VEC, vecs_d, d1, (), [bVEC])
        d2 = dsem()
        DMA("pool", CM, cmat_d.rearrange("p (a b) -> p a b", b=128), d2, (), [bCM])
        MEMSET("dve", ONESF, 1.0, [bONESF])

        cv_ap = VEC[:, VOFF["cvec"]:VOFF["cvec"] + 16].rearrange("p (a b) -> p a b", b=2)
        ACT(ST, cv_ap, AF.Silu, [bVEC], [bST])
        def mod_plan(i, WMl, bWMl, dWMl):
            pm, bpm = bank(reserve=True)
            bi = (state["bank"] - 1) % 8
            pmv = pm[:, 0:96].rearrange("p (a b) -> p a b", b=2)
            nr = len(WMl)

            def dma(pc):
                r = pc % nr
                row = (i * 12 + pc) * 128
                DMA("pool", WMl[r], wmod_d[row:row + 128, :].rearrange("p (a b) -> p a b", b=512),
                    dWMl[r], (), [bWMl[r]])

            def mms(pc):
                r = pc % nr
                for q in range(4):
                    cc = pc * 4 + q
                    for kc in range(8):
                        MM(pmv[:, cc, :], WMl[r][:, kc, q * 128:(q + 1) * 128], ST[:, kc, :],
                           kc == 0, kc == 7, [bWMl[r], bST], [bpm])

            def fin():
                bm = VEC[:, VOFF["bmod"] + i * 48:VOFF["bmod"] + (i + 1) * 48]
                for s_ in range(2):
                    TT("dve", MOD[:, i, :, :, s_], pmv[:, :, s_].rearrange("p (a b) -> p a b", b=8),
                       bm.rearrange("p (a b) -> p a b", b=8), ALU.add, [bpm, bVEC], [bMODL[i]])
                reserved.discard(bi)
            return dma, mms, fin

        XSp = [alloc([15, 64], F32) for _ in range(2)]
        bXSp = [Buf(), Buf()]
        dXSp = [dsem(), dsem()]
        TBp = [alloc([2, NBLK, 64], BF16) for _ in range(2)]
        bTBp = [Buf(), Buf()]
        dTABo = [dsem(), dsem()]
        bTABd = [[Buf() for _ in range(8)] for _ in range(2)]
        for r_ in range(2):
            MEMSET("dve", TBp[r_], 0.0, [bTBp[r_]])
        for u_ in range(16):
            j_, h_ = u_ // 8, u_ % 8
            rx = u_ % 2
            row = (j_ * 8 + h_) * 64
            for half in range(2):
                DMA("sp", XSp[rx][half * 64:(half + 1) * 64],
                    rpbx_d[row:row + 64, :].rearrange("p (a b) -> p a b", b=64), dXSp[rx], (), [bXSp[rx]])
            ACT(TBp[rx][0:64, 0, DD - 7:DD + 8, :], XSp[rx][0:64], AF.Exp, [bXSp[rx]], [bTBp[rx]])
            ACT(TBp[rx][64:128, 0, DD - 6:DD + 9, :], XSp[rx][64:128], AF.Exp, [bXSp[rx]], [bTBp[rx]])
            ACT(TBp[rx][0:64, 1, DD - 3:DD + 5, :], XSp[rx][0:64, 4:12, :], AF.Exp, [bXSp[rx]], [bTBp[rx]])
            ACT(TBp[rx][64:128, 1, DD - 2:DD + 6, :], XSp[rx][64:128, 4:12, :], AF.Exp, [bXSp[rx]], [bTBp[rx]])
            DMA("sp", tab_d[u_ * 128:(u_ + 1) * 128, :], TBp[rx].rearrange("p a b c -> p (a b c)"), dTABo[rx],
                [bTBp[rx]], [bTABd[j_][h_]])
        WM = [alloc([8, 512], BF16) for _ in range(2)]
        bWM = [Buf(), Buf()]
        dWM = [dsem(), dsem()]
        dma0, mms0, fin0 = mod_plan(0, WM, bWM, dWM)
        dma0(0)
        for pc in range(12):
            if pc + 1 < 12:
                dma0(pc + 1)
            mms0(pc)
        fin0()
        S.barrier()
        state["top"] = PERSIST_TOP

        def modv(i, m, kc, s):
            return MOD[:, i, m, kc, s:s + 1]

        def layer_vectors(i):
            nm = VEC[:, VOFF["nmix"] + i * 8:VOFF["nmix"] + (i + 1) * 8]
            nf = VEC[:, VOFF["nffn"] + i * 8:VOFF["nffn"] + (i + 1) * 8]
            for s in range(2):
                STT("dve", AV[:, 0, :, s], MOD[:, i, 1, :, s], 1.0, nm, ALU.add, ALU.mult,
                    [bMODL[i], bVEC], [bAV])
                STT("dve", AV[:, 1, :, s], MOD[:, i, 4, :, s], 1.0, nf, ALU.add, ALU.mult,
                    [bMODL[i], bVEC], [bAV])

        def norm(i, which, H, HB):
            SQ = [alloc([8, 512], BF16) for _ in range(2)]
            bSQ = [Buf(), Buf()]
            SD = [alloc([512], F32) for _ in range(2)]
            RS = [alloc([512], F32) for _ in range(2)]
            bSD = [Buf(), Buf()]
            bRS = [Buf(), Buf()]
            TMP = [alloc([512], F32) for _ in range(4)]
            bTMP = [Buf() for _ in range(4)]
            bm = 0 if which == 0 else 3
            k = 0
            for ti, (t0, n, s) in enumerate(TL):
                r = ti % 2
                for kc in range(8):
                    ACT(SQ[r][:, kc, :], X[:, kc, t0:t0 + n], AF.Square, [XB[kc][ti]], [bSQ[r]])
                pa, bpa = bank()
                for kc in range(8):
                    MM(pa, ONES1024, SQ[r][:, kc, :], kc == 0, kc == 7, [bSQ[r], bCM], [bpa])
                ACT(SD[r], pa, AF.Ln, [bpa, bEPS], [bSD[r]], bias=EPSV, scale=1.0)
                ACT(RS[r], SD[r], AF.Exp, [bSD[r]], [bRS[r]], scale=-0.5)
                for kc in range(8):
                    q = k % 4
                    k += 1
                    TT("dve", TMP[q], X[:, kc, t0:t0 + n], RS[r], ALU.mult, [XB[kc][ti], bRS[r]], [bTMP[q]])
                    ACT(H[:, kc, t0:t0 + n], TMP[q], AF.Identity, [bTMP[q], bAV, bMODL[i]], [HB[kc][ti]],
                        bias=modv(i, bm, kc, s), scale=AV[:, which, kc, s:s + 1])

        def resid_add(pso, bpso, i, gm, oc, ti):
            t0, n, s = TL[ti]
            STT("dve", X[:, oc, t0:t0 + n], pso, modv(i, gm, oc, s), X[:, oc, t0:t0 + n],
                ALU.mult, ALU.add, [bpso, bMODL[i]], [XB[oc][ti]])

        EPSV = alloc([1], F32)
        bEPS = Buf()
        MEMSET("dve", EPSV, EPS, [bEPS])
        PERSIST_TOP = state["top"]
        S.barrier()

        def ffn(i, H, HB):
            WI = [alloc([8, 512], BF16) for _ in range(2)]
            WO = [alloc([2, 1024], BF16) for _ in range(4)]
            bWI = [Buf(), Buf()]
            bWO = [Buf() for _ in range(4)]
            dWI = [dsem(), dsem()]
            dWO = [dsem() for _ in range(4)]
            ACTB = [alloc([2, NT], BF16) for _ in range(3)]
            bACT = [[[Buf() for _ in range(5)] for _ in range(2)] for _ in range(3)]
            NY = 5
            YB = [alloc([512], F32) for _ in range(NY)]
            bY = [Buf() for _ in range(NY)]
            yk = {"k": 0}

            def ybuf():
                q = yk["k"] % NY
                yk["k"] += 1
                return YB[q], bY[q]

            def load(pc):
                row = (i * NPIECE + pc) * 128
                DMA("pool", WI[pc % 2], fwi_d[row:row + 128, :].rearrange("p (a b) -> p a b", b=512),
                    dWI[pc % 2], (), [bWI[pc % 2]])
                DMA("pool", WO[pc % 4], fwo_d[row:row + 128, :].rearrange("p (a b) -> p a b", b=1024),
                    dWO[pc % 4], (), [bWO[pc % 4]])

            def up(pc):
                r = pc % 2
                ra = pc % 3
                for f in range(2):
                    fc = pc * 2 + f
                    prev = [None, None]
                    ys = {}

                    def finish(ti):
                        t0, n, s = TL[ti]
                        ya, bya = ys.pop((0, ti))
                        yg, byg = ys.pop((1, ti))
                        ACT(ya, ya, AF.Silu, [bya], [bya])
                        TT("pool", ACTB[ra][:, f, t0:t0 + n], ya, yg, ALU.mult, [bya, byg], [bACT[ra][f][ti]])

                    for ti, (t0, n, s) in enumerate(TL):
                        for br in range(2):
                            ch = fc + 22 * br
                            w0, w1, w2 = (vec("fcw", i, ch, 0), vec("fcw", i, ch, 1), vec("fcw", i, ch, 2))
                            bb = vec("fcb", i, ch)
                            col = (f * 2 + br) * 128
                            pu, bpu = bank()
                            for kc in range(8):
                                MM(pu, WI[r][:, kc, col:col + 128], H[:, kc, t0:t0 + n], kc == 0, kc == 7,
                                   [bWI[r], HB[kc][ti]], [bpu])
                            y, by = ybuf()
                            ys[(br, ti)] = (y, by)
                            ACT(y, pu, AF.Identity, [bpu, bVEC], [by], bias=bb, scale=w1)
                            if s == 0:
                                rngs = [(0, 512)]
                            else:
                                rngs = [(0, 256), (256, 512)]
                            for (a, b) in rngs:
                                STT("dve", y[:, a + 1:b], pu[:, a:b - 1], w0, y[:, a + 1:b], ALU.mult, ALU.add,
                                    [bpu, by, bVEC], [by])
                                STT("dve", y[:, a:b - 1], pu[:, a + 1:b], w2, y[:, a:b - 1], ALU.mult, ALU.add,
                                    [bpu, by, bVEC], [by])
                            if s == 0 and ti > 0:
                                ppu, bppu, py, bpy = prev[br]
                                STT("dve", y[:, 0:1], ppu[:, 511:512], w0, y[:, 0:1], ALU.mult, ALU.add,
                                    [bppu, by, bVEC], [by])
                                STT("dve", py[:, 511:512], pu[:, 0:1], w2, py[:, 511:512], ALU.mult, ALU.add,
                                    [bpu, bpy, bVEC], [bpy])
                            prev[br] = (pu, bpu, y, by)
                        if s == 0 and ti > 0:
                            finish(ti - 1)
                        if (s == 0 and ti == 3) or s == 1:
                            finish(ti)

            def down(pcs):
                nmm = 2 * len(pcs)
                for ti, (t0, n, s) in enumerate(TL):
                    for oc in range(8):
                        po, bpo = bank()
                        k_ = 0
                        for pc in pcs:
                            for f in range(2):
                                MM(po, WO[pc % 4][:, f, oc * 128:(oc + 1) * 128], ACTB[pc % 3][:, f, t0:t0 + n],
                                   k_ == 0, k_ == nmm - 1, [bWO[pc % 4], bACT[pc % 3][f][ti]], [bpo])
                                k_ += 1
                        resid_add(po, bpo, i, 5, oc, ti)

            if i + 1 < 4:
                WMf = [alloc([8, 512], BF16)]
                mdma, mmms, mfin = mod_plan(i + 1, WMf, [Buf()], [dsem()])
                mdma(0)
            load(0)
            for pc in range(NPIECE):
                if pc + 1 < NPIECE:
                    load(pc + 1)
                if i + 1 < 4:
                    mmms(pc)
                    mdma(pc + 1)
                    if pc == NPIECE - 1:
                        mmms(pc + 1)
                up(pc)
                if pc >= 2 and pc % 2 == 0:
                    down([pc - 2, pc - 1])
            down([NPIECE - 1])
            if i + 1 < 4:
                mfin()

        LG = 2650
        GB = [15, 2093, 2379]

        def conformer(i, H, HB):
            j = i // 2
            top0 = state["top"]
            GLU = alloc([8, LG], BF16)
            bGLU = [Buf() for _ in range(8)]
            for c in range(8):
                MEMSET("dve", GLU[:, c, :], 0.0, [bGLU[c]])
            W1 = [alloc([8, 256], BF16) for _ in range(2)]
            bW1 = [Buf(), Buf()]
            dW1 = [dsem(), dsem()]
            SG = [alloc([512], F32) for _ in range(3)]
            bSG = [Buf() for _ in range(3)]
            k = 0
            for oc in range(8):
                r = oc % 2
                row = (j * 8 + oc) * 128
                DMA("pool", W1[r], pw1_d[row:row + 128, :].rearrange("p (a b) -> p a b", b=256),
                    dW1[r], (), [bW1[r]])
                for ti, (t0, n, s) in enumerate(TL):
                    pa, bpa = bank()
                    pg, bpg = bank()
                    for kc in range(8):
                        MM(pa, W1[r][:, kc, 0:128], H[:, kc, t0:t0 + n], kc == 0, kc == 7, [bW1[r], HB[kc][ti]], [bpa])
                    for kc in range(8):
                        MM(pg, W1[r][:, kc, 128:256], H[:, kc, t0:t0 + n], kc == 0, kc == 7, [bW1[r], HB[kc][ti]], [bpg])
                    q = k % 3
                    k += 1
                    ACT(SG[q], pg, AF.Sigmoid, [bpg], [bSG[q]])
                    if s == 0:
                        TT("dve", GLU[:, oc, GB[0] + t0:GB[0] + t0 + n], pa, SG[q], ALU.mult, [bpa, bSG[q]], [bGLU[oc]])
                    else:
                        for c2 in range(2):
                            TT("dve", GLU[:, oc, GB[1 + c2]:GB[1 + c2] + 256], pa[:, c2 * 256:(c2 + 1) * 256],
                               SG[q][:, c2 * 256:(c2 + 1) * 256], ALU.mult, [bpa, bSG[q]], [bGLU[oc]])
            S.barrier()
            state["top"] = top0
            GLU2 = alloc([8, LG], BF16)
            VB = H
            bVB = [[Buf() for _ in range(5)] for _ in range(8)]
            DG = [alloc([31, 128], BF16) for _ in range(2)]
            bDG = [Buf(), Buf()]
            dDG = [dsem(), dsem()]
            for c in range(8):
                r = c % 2
                row = (j * 8 + c) * 128
                DMA("pool", DG[r], dwd_d[row:row + 128, :].rearrange("p (a b) -> p a b", b=128),
                    dDG[r], (), [bDG[r]])
                for ti, (t0, n, s) in enumerate(TL):
                    pv, bpv = bank()
                    if s == 0:
                        for jj in range(31):
                            st = GB[0] + t0 + jj - 15
                            MM(pv, DG[r][:, jj, :], GLU2[:, c, st:st + 512], jj == 0, jj == 30, [bDG[r], bGLU[c]], [bpv])
                    else:
                        for c2 in range(2):
                            for jj in range(31):
                                st = GB[1 + c2] + jj - 15
                                MM(pv[:, c2 * 256:(c2 + 1) * 256], DG[r][:, jj, :], GLU2[:, c, st:st + 256],
                                   jj == 0, jj == 30, [bDG[r], bGLU[c]], [bpv])
                    ACT(VB[:, c, t0:t0 + n], pv, AF.Identity, [bpv, bVEC], [bVB[c][ti]], bias=vec("dwb", j, c), scale=1.0)
            S.barrier()
            state["top"] = top0
            W2 = alloc([8, 1024], BF16)
            bW2 = Buf()
            dW2 = dsem()
            DMA("pool", W2, pw2_d[j * 128:(j + 1) * 128, :].rearrange("p (a b) -> p a b", b=1024), dW2, (), [bW2])
            SQ = [alloc([8, 512], BF16) for _ in range(2)]
            bSQ = [Buf(), Buf()]
            SS = [alloc([8, 512], BF16) for _ in range(2)]
            bSS = [[Buf() for _ in range(8)] for _ in range(2)]
            MS = [alloc([512], F32) for _ in range(2)]
            bMS = [Buf(), Buf()]
            M2 = [alloc([512], F32) for _ in range(2)]
            bM2 = [Buf(), Buf()]
            RS = [alloc([512], F32) for _ in range(2)]
            bRS = [Buf(), Buf()]
            T1 = [alloc([512], F32) for _ in range(4)]
            bT1 = [Buf() for _ in range(4)]
            k = 0
            for ti, (t0, n, s) in enumerate(TL):
                r = ti % 2
                for c in range(8):
                    ACT(SQ[r][:, c, :], VB[:, c, t0:t0 + n], AF.Square, [bVB[c][ti]], [bSQ[r]])
                pm, bpm = bank()
                pq, bpq = bank()
                for c in range(8):
                    MM(pm, ONES1024, VB[:, c, t0:t0 + n], c == 0, c == 7, [bVB[c][ti], bCM], [bpm])
                for c in range(8):
                    MM(pq, ONES1024, SQ[r][:, c, :], c == 0, c == 7, [bSQ[r], bCM], [bpq])
                ACT(MS[r], pm, AF.Identity, [bpm], [bMS[r]])
                TT("dve", M2[r], MS[r], MS[r], ALU.mult, [bMS[r]], [bM2[r]])
                TT("dve", M2[r], pq, M2[r], ALU.subtract, [bpq, bM2[r]], [bM2[r]])
                ACT(M2[r], M2[r], AF.Ln, [bM2[r], bEPS], [bM2[r]], bias=EPSV, scale=1.0)
                ACT(RS[r], M2[r], AF.Exp, [bM2[r]], [bRS[r]], scale=-0.5)
                for c in range(8):
                    q = k % 4
                    k += 1
                    TT("dve", T1[q], VB[:, c, t0:t0 + n], MS[r], ALU.subtract, [bVB[c][ti], bMS[r]], [bT1[q]])
                    TT("dve", T1[q], T1[q], RS[r], ALU.mult, [bT1[q], bRS[r]], [bT1[q]])
                    ACT(SS[r][:, c, :], T1[q], AF.Silu, [bT1[q], bVEC], [bSS[r][c]],
                        bias=vec("lnb", j, c), scale=vec("lng", j, c))
                for oc in range(8):
                    po, bpo = bank()
                    for c in range(8):
                        MM(po, W2[:, c, oc * 128:(oc + 1) * 128], SS[r][:, c, :], c == 0, c == 7, [bW2, bSS[r][c]], [bpo])
                    resid_add(po, bpo, i, 2, oc, ti)
            S.barrier()
            state["top"] = top0

        def even_mixer(i, H, HB):
            j = i // 2
            top0 = state["top"]
            WP = alloc([8, 512], BF16)
            bWP = Buf()
            dWP = dsem()
            for pc in range(2):
                row = (j * 8 + 6 + pc) * 128
                DMA("pool", WP[:, :, pc * 256:(pc + 1) * 256],
                    winab_d[row:row + 128, :].rearrange("p (a b) -> p a b", b=256), dWP, (), [bWP])
            PW = alloc([4, 128], BF16)
            WOP = alloc([4, 1024], BF16)
            bPW, bWOP = Buf(), Buf()
            DMA("pool", PW, poolw_d[j * 128:(j + 1) * 128, :].rearrange("p (a b) -> p a b", b=128), dsem(), (), [bPW])
            DMA("pool", WOP, wop_d[j * 128:(j + 1) * 128, :].rearrange("p (a b) -> p a b", b=1024), dsem(), (), [bWOP])
            PTM = alloc([20, 512], BF16)
            bPTM = [Buf() for _ in range(20)]
            for tt in range(20):
                ti = tt // 4
                pp, bpp = bank()
                for kc in range(8):
                    MM(pp, H[:, kc, tt * 128:(tt + 1) * 128], WP[:, kc, :], kc == 0, kc == 7, [HB[kc][ti], bWP], [bpp])
                ACT(PTM[:, tt, :], pp, AF.Identity, [bpp], [bPTM[tt]])
            BND = [alloc([6, 512], BF16) for _ in range(3)]
            bBND = [Buf(), Buf(), Buf()]
            dBND = [dsem(), dsem(), dsem()]
            DT = [alloc([512], BF16) for _ in range(2)]
            bDT = [Buf(), Buf()]
            YP = [alloc([4, 512], BF16) for _ in range(2)]
            bYP = [[Buf() for _ in range(4)] for _ in range(2)]
            k = 0
            for ti, (t0, n, s) in enumerate(TL):
                ry = ti % 2
                for g in range(4):
                    r = k % 3
                    k += 1
                    pd, bpd = bank()
                    if s == 0:
                        row = (g * 4 + ti) * 128
                        DMA("pool", BND[r], band_d[row:row + 128, :].rearrange("p (a b) -> p a b", b=512),
                            dBND[r], (), [bBND[r]])
                        its = [it for it in range(4 * ti - 1, 4 * ti + 5) if 0 <= it < 16]
                        for n_, it in enumerate(its):
                            MM(pd, PTM[:, it, g * 128:(g + 1) * 128], BND[r][:, it - (4 * ti - 1), :],
                               n_ == 0, n_ == len(its) - 1, [bPTM[it], bBND[r]], [bpd])
                    else:
                        DMA("pool", BND[r][:, 0:2, 0:256],
                            bandc_d[g * 128:(g + 1) * 128, :].rearrange("p (a b) -> p a b", b=256),
                            dBND[r], (), [bBND[r]])
                        for c2 in range(2):
                            for it2 in range(2):
                                it = 16 + 2 * c2 + it2
                                MM(pd[:, c2 * 256:(c2 + 1) * 256], PTM[:, it, g * 128:(g + 1) * 128],
                                   BND[r][:, it2, 0:256], it2 == 0, it2 == 1, [bPTM[it], bBND[r]], [bpd])
                    rd = k % 2
                    ACT(DT[rd], pd, AF.Identity, [bpd], [bDT[rd]])
                    py, bpy = bank()
                    MM(py, PW[:, g, :], DT[rd], True, True, [bPW, bDT[rd]], [bpy])
                    ACT(YP[ry][:, g, :], py, AF.Identity, [bpy, bVEC], [bYP[ry][g]], scale=vec("pscale", j, g))
                for oc in range(8):
                    po, bpo = bank()
                    for g in range(4):
                        MM(po, WOP[:, g, oc * 128:(oc + 1) * 128], YP[ry][:, g, :], g == 0, g == 3,
                           [bWOP, bYP[ry][g]], [bpo])
                    resid_add(po, bpo, i, 2, oc, ti)
            S.barrier()
            state["top"] = top0
            if stop_after == "pool":
                return
            h_lo = top0 - 8 * NT * 2
            KT = alloc([4, NT], BF16)
            bKT = [[Buf() for _ in range(5)] for _ in range(4)]
            VT = alloc([24, 8, 66], BF16)
            bVT = [Buf() for _ in range(24)]
            CK = alloc([4, 512], BF16)
            bCK = Buf()
            top_keep = state["top"]
            MEMSET("dve", VT[:, :, :, 64:66], 1.0, [bVT[tt] for tt in range(24)])
            DMA("pool", CK, ck_d[j * 128:(j + 1) * 128, :].rearrange("p (a b) -> p a b", b=512), dsem(), (), [bCK])
            dcv = dsem()
            for t in range(4):
                DMA("pool", VT[:, 20 + t, :, 0:64],
                    cv_d[j * 128:(j + 1) * 128, t * 512:(t + 1) * 512].rearrange("p (a b) -> p a b", b=64),
                    dcv, (), [bVT[20 + t]])
            if stop_after == "qkvA":
                S.barrier()
                return
            WQ = [alloc([8, 256], BF16) for _ in range(2)]
            bWQ = [Buf(), Buf()]
            dWQ = [dsem(), dsem()]
            SQ1 = [alloc([512], BF16) for _ in range(2)]
            bSQ1 = [Buf(), Buf()]
            SD = [alloc([512], F32) for _ in range(2)]
            bSD = [Buf(), Buf()]
            RS = [alloc([512], F32) for _ in range(2)]
            bRS = [Buf(), Buf()]
            QST = [alloc([512], BF16) for _ in range(3)]
            bQST = [Buf() for _ in range(3)]
            KOUT = [alloc([512], F32) for _ in range(2)]
            bKOUT = [Buf(), Buf()]
            bQS = [[Buf() for _ in range(5)] for _ in range(4)]
            dQS = dsem()
            k = 0
            kq = 0
            ko = 0
            for pc in range(4):
                r = pc % 2
                row = (j * 8 + pc) * 128
                DMA("pool", WQ[r], winab_d[row:row + 128, :].rearrange("p (a b) -> p a b", b=256),
                    dWQ[r], (), [bWQ[r]])
                for ti, (t0, n, s) in enumerate(TL):
                    for c2 in range(2):
                        ch = (pc % 2) * 2 + c2
                        pq, bpq = bank()
                        for kc in range(8):
                            MM(pq, WQ[r][:, kc, c2 * 128:(c2 + 1) * 128], H[:, kc, t0:t0 + n], kc == 0, kc == 7,
                               [bWQ[r], HB[kc][ti]], [bpq])
                        q = k % 2
                        k += 1
                        ACT(SQ1[q], pq, AF.Square, [bpq], [bSQ1[q]])
                        pn, bpn = bank()
                        MM(pn, BLK64, SQ1[q], True, True, [bCM, bSQ1[q]], [bpn])
                        ACT(SD[q], pn, AF.Ln, [bpn, bEPS], [bSD[q]], bias=EPSV, scale=1.0)
                        ACT(RS[q], SD[q], AF.Exp, [bSD[q]], [bRS[q]], scale=-0.5)
                        if pc < 2:
                            qq = kq % 3
                            kq += 1
                            STT("dve", QST[qq], pq, QG8, RS[q], ALU.mult, ALU.mult, [bpq, bQG8, bRS[q]], [bQST[qq]])
                            DMA("sp", qs_d[:, ch * NT + t0:ch * NT + t0 + n], QST[qq], dQS, [bQST[qq]], [bQS[ch][ti]])
                        else:
                            kgv = VEC[:, VOFF["kg"] + j:VOFF["kg"] + j + 1]
                            STT("dve", KT[:, ch, t0:t0 + n], pq, kgv, RS[q], ALU.mult, ALU.mult,
                                [bpq, bVEC, bRS[q]], [bKT[ch][ti]])
                            if s == 1:
                                o_ = ko % 2
                                ko += 1
                                STT("dve", KOUT[o_], pq, kgv, RS[q], ALU.mult, ALU.mult,
                                    [bpq, bVEC, bRS[q]], [bKOUT[o_]])
                                DMA("sp", nk_d[j * 128:(j + 1) * 128, ch * 512:(ch + 1) * 512], KOUT[o_], dout,
                                    [bKOUT[o_]], [])
            if stop_after == "qkvB":
                S.barrier()
                return
            for pc in range(2):
                row = (j * 8 + 4 + pc) * 128
                DMA("pool", WQ[pc], winab_d[row:row + 128, :].rearrange("p (a b) -> p a b", b=256),
                    dWQ[pc], (), [bWQ[pc]])
            VOUT = KOUT
            bVOUT = bKOUT
            for tt in range(20):
                ti = tt // 4
                pv, bpv = bank()
                for pc in range(2):
                    for kc in range(8):
                        MM(pv[:, pc * 256:(pc + 1) * 256], H[:, kc, tt * 128:(tt + 1) * 128], WQ[pc][:, kc, :],
                           kc == 0, kc == 7, [HB[kc][ti], bWQ[pc]], [bpv])
                if stop_after != "qkvC1":
                    for hh in range(8):
                        ACT(VT[:, tt, hh, 0:64], pv[:, hh * 64:(hh + 1) * 64], AF.Identity, [bpv], [bVT[tt]])
                if tt >= 16:
                    o_ = tt % 2
                    S.op("dve", (lambda o_=o_, pv=pv: (lambda e: e.tensor_copy(out=VOUT[o_], in_=pv)))(),
                         [bpv], [bVOUT[o_]])
                    DMA("sp", nv_d[j * 128:(j + 1) * 128, (tt - 16) * 512:(tt - 15) * 512], VOUT[o_], dout,
                        [bVOUT[o_]], [])
            S.barrier()
            if stop_after in ("qkv", "qkvC1"):
                return
            state["top"] = h_lo
            state["skip"] = (top0, top_keep)
            WOA = alloc([8, 1024], BF16)
            bWOA = Buf()
            MEMSET("dve", WOA[64:128], 0.0, [bWOA])
            DMA("pool", WOA[0:64], woa_d[j * 64:(j + 1) * 64, :].rearrange("p (a b) -> p a b", b=1024),
                dsem(), (), [bWOA])
            AT0 = alloc([8, 512], BF16)
            bAT0 = [Buf() for _ in range(8)]
            MEMSET("dve", AT0, 0.0, bAT0)
            AT = [AT0, AT0]
            bAT = [bAT0, bAT0]
            QZ = [alloc([8, 512], BF16) for _ in range(2)]
            bQT = [Buf(), Buf()]
            dQT = [dsem(), dsem()]
            for r_ in range(2):
                MEMSET("dve", QZ[r_], 0.0, [bQT[r_]])
            TBL = [alloc([2, NBLK, 64], BF16) for _ in range(2)]
            bTB = [Buf(), Buf()]
            dTB = [dsem(), dsem()]
            NE = 8
            EB = [alloc([512], BF16) for _ in range(NE)]
            bEB = [Buf() for _ in range(NE)]
            OS = [alloc([512], F32) for _ in range(2)]
            bOS = [Buf(), Buf()]
            RC = [alloc([512], F32) for _ in range(2)]
            bRC = [Buf(), Buf()]
            ek = {"k": 0, "hk": 0, "mk": 0}

            def ebuf():
                q = ek["k"] % NE
                ek["k"] += 1
                return EB[q], bEB[q]

            def meng():
                ek["mk"] += 1
                return "dve"

            def finalize(po, bpo, ncol, at, bat, h, c0):
                q = ek["hk"] % 2
                ek["hk"] += 1
                ACT(OS[q][0:65, 0:ncol], po[0:65, 0:ncol], AF.Identity, [bpo], [bOS[q]])
                RECIP(RC[q][64:65, 0:ncol], OS[q][64:65, 0:ncol], [bOS[q]], [bRC[q]])
                pb, bpb = bank()
                MM(pb[0:64, 0:ncol], ONESF[64:65, 0:64], RC[q][64:65, 0:ncol], True, True, [bONESF, bRC[q]], [bpb])
                TT("dve", at[0:64, h, c0:c0 + ncol], OS[q][0:64, 0:ncol], pb[0:64, 0:ncol], ALU.mult,
                   [bOS[q], bpb], [bat[h]])

            from collections import deque
            LOOK = 5
            DEFER = 5
            units = []
            for ti, (t0, n, s) in enumerate(TL):
                for h in range(8):
                    if s == 0:
                        units.append((ti, h, None))
                    else:
                        units.append((ti, h, "c"))
            steps = []
            for ui, (ti, h, c2) in enumerate(units):
                if c2 is None:
                    r0 = 8 * ti
                    rs0 = min(max(r0 - 4, 0), 24)
                    rs7 = min(max(r0 + 7 - 4, 0), 24)
                    tl = [("cache", t) for t in range(4)] + [("local", kr0) for kr0 in range(rs0, rs7 + 8, 2)]
                else:
                    tl = [("ctx", (cq, t)) for cq in range(2) for t in range(2)]
                for k_, d_ in enumerate(tl):
                    steps.append((ui, d_, k_ == 0, k_ == len(tl) - 1))
            local_units = [ui for ui, u in enumerate(units) if u[2] is None]
            recs = {}
            ust = {}
            pend = deque()
            reserved.clear()

            def load_q(ti):
                t0, n, s = TL[ti]
                rq = ti % 2
                qv = QZ[rq].rearrange("p (c two) n -> p c two n", two=2)
                for half in range(2):
                    src = qs_d[half * 64:(half + 1) * 64, :].rearrange("p (c t) -> p c t", t=NT)
                    DMA("sp", qv[half * 64:(half + 1) * 64, :, half, :], src[:, :, t0:t0 + n], dQT[rq],
                        [bQS[ch][ti] for ch in range(4)], [bQT[rq]])

            def load_tab(ui):
                ti, h, c2 = units[ui]
                rx = local_units.index(ui) % 2
                DMA("sp", TBL[rx].rearrange("p a b c -> p (a b c)"),
                    tab_d[(j * 8 + h) * 128:(j * 8 + h + 1) * 128, :], dTB[rx], [bTABd[j][h]], [bTB[rx]])

            utab = {}

            def unit_start(ui):
                ti, h, c2 = units[ui]
                if h == 0:
                    if ti == 0:
                        load_q(0)
                    if ti + 1 < len(TL):
                        load_q(ti + 1)
                if c2 is None:
                    li = local_units.index(ui)
                    if li == 0:
                        load_tab(ui)
                    if li + 1 < len(local_units):
                        load_tab(local_units[li + 1])
                    utab[ui] = li % 2

            def emit_S(idx):
                ui, (kind, arg), first, last = steps[idx]
                ti, h, c2 = units[ui]
                rq = ti % 2
                c = h // 2
                p0 = (h % 2) * 64
                psS, bps = bank()
                eb, beb = ebuf()
                if kind == "ctx":
                    cq, t = arg
                    tt = 16 + 2 * cq + t
                    MM(psS[:, 0:256], KT[:, c, tt * 128:(tt + 1) * 128],
                       QZ[rq][:, h, cq * 256:(cq + 1) * 256], True, True, [bKT[c][4], bQT[rq]], [bps])
                    ACT(eb[:, 0:256], psS[:, 0:256], AF.Exp, [bps], [beb])
                    recs[idx] = (eb[:, 0:256], beb, tt, 256, cq * 256, t == 0, t == 1)
                    return
                if kind == "cache":
                    t = arg
                    MM(psS, CK[:, c, t * 128:(t + 1) * 128], QZ[rq][:, h, :], True, True,
                       [bCK, bQT[rq]], [bps])
                    ACT(eb, psS, AF.Exp, [bps], [beb])
                    recs[idx] = (eb, beb, 20 + t, 512, 0, None, None)
                    return
                kr0 = arg
                r0 = 8 * ti
                rt = utab[ui]
                TE_, TI_, btb = TBL[rt][:, 0], TBL[rt][:, 1], bTB[rt]
                MM(psS, KT[:, c, kr0 * 64:kr0 * 64 + 128], QZ[rq][:, h, :], True, True,
                   [bKT[c][kr0 // 8], bQT[rq]], [bps])
                ACT(eb, psS, AF.Exp, [bps], [beb])

                def tslice(tab, qa, nrow):
                    j0 = DD - (kr0 - qa)
                    return tab[:, j0:j0 + nrow, :]

                def ev(a_, b_):
                    return eb[:, a_ * 64:b_ * 64].rearrange("p (a b) -> p a b", b=64)
                if ti in (1, 2):
                    TT("dve", ev(0, 8), ev(0, 8), tslice(TI_, r0, 8), ALU.mult, [beb, btb], [beb])
                elif ti == 0:
                    TT("dve", ev(4, 8), ev(4, 8), tslice(TI_, 4, 4), ALU.mult, [beb, btb], [beb])
                    if kr0 < 8:
                        TT("dve", ev(0, 4), ev(0, 4), tslice(TE_, 0, 4), ALU.mult, [beb, btb], [beb])
                    else:
                        MEMSET("dve", eb[:, 0:256], 0.0, [beb])
                else:
                    TT("dve", ev(0, 5), ev(0, 5), tslice(TI_, 24, 5), ALU.mult, [beb, btb], [beb])
                    if kr0 >= 24:
                        TT("dve", ev(5, 8), ev(5, 8), tslice(TE_, 29, 3), ALU.mult, [beb, btb], [beb])
                    else:
                        MEMSET("dve", eb[:, 320:512], 0.0, [beb])
                recs[idx] = (eb, beb, kr0 // 2, 512, 0, None, None)

            def out_proj(ti):
                rq = ti % 2
                at, bat = AT[rq], bAT[rq]
                for oc in range(8):
                    po2, bpo2 = bank()
                    for h in range(8):
                        MM(po2, WOA[:, h, oc * 128:(oc + 1) * 128], at[:, h, :], h == 0, h == 7,
                           [bWOA, bat[h]], [bpo2])
                    resid_add(po2, bpo2, i, 2, oc, ti)

            def emit_PV(idx, now):
                ui, (kind, arg), first, last = steps[idx]
                ti, h, c2 = units[ui]
                eb, beb, tt, ncol, cofs, st_, sp_ = recs.pop(idx)
                st_ = first if st_ is None else st_
                sp_ = last if sp_ is None else sp_
                if first:
                    i_ = state["bank"]
                    po, bpo = bank(reserve=True)
                    ust[ui] = (po, bpo, (state["bank"] - 1) % 8)
                po, bpo, bi = ust[ui]
                vflat = VT[:, tt, :, :].rearrange("p a b -> p (a b)")
                mw = min(128, 8 * 66 - h * 66)
                MM(po[0:mw, cofs:cofs + ncol], vflat[:, h * 66:h * 66 + mw], eb, st_, sp_, [bVT[tt], beb], [bpo])
                if last:
                    q = ek["hk"] % 2
                    ek["hk"] += 1
                    rq = ti % 2
                    at, bat = AT[rq], bAT[rq]
                    c0 = 0
                    ncol = 512
                    ACT(OS[q][0:65, 0:ncol], po[0:65, 0:ncol], AF.Identity, [bpo], [bOS[q]])
                    ACT(RC[q][64:65, 0:ncol], OS[q][64:65, 0:ncol], AF.Ln, [bOS[q]], [bRC[q]])
                    ACT(RC[q][64:65, 0:ncol], RC[q][64:65, 0:ncol], AF.Exp, [bRC[q]], [bRC[q]], scale=-1.0)

                    def stage_b(q=q, ncol=ncol, at=at, bat=bat, h=h, c0=c0, bi=bi, ti=ti, c2=c2, ui=ui):
                        pb, bpb = bank()
                        MM(pb[0:64, 0:ncol], ONESF[64:65, 0:64], RC[q][64:65, 0:ncol], True, True,
                           [bONESF, bRC[q]], [bpb])
                        TT("dve", at[0:64, h, c0:c0 + ncol], OS[q][0:64, 0:ncol], pb[0:64, 0:ncol], ALU.mult,
                           [bOS[q], bpb], [bat[h]])
                        reserved.discard(bi)
                        del ust[ui]
                        if h == 7:
                            out_proj(ti)
                    pend.append((now + (DEFER if c2 is None else 3), stage_b))

            nst = len(steps)
            for idx in range(nst + LOOK + DEFER + 2):
                while pend and pend[0][0] <= idx:
                    pend.popleft()[1]()
                if idx < nst:
                    if steps[idx][2]:
                        unit_start(steps[idx][0])
                    emit_S(idx)
                jn = idx - LOOK
                if 0 <= jn < nst and steps[jn][2] and units[steps[jn][0]][2] == "c" and units[steps[jn][0]][1] == 0:
                    while pend:
                        pend.popleft()[1]()
                jdx = idx - LOOK
                if 0 <= jdx < nst:
                    emit_PV(jdx, idx)
            assert not pend and not ust and not recs
            reserved.clear()
            S.barrier()
            state["skip"] = None
            state["top"] = top0

        done = False
        for i in range(4):
            if done or stop_after == "mod":
                break
            layer_vectors(i)
            if i % 2 == 0:
                j = i // 2
                S.op("act", (lambda j=j: (lambda e: e.mul(QG8, VEC[:, VOFF["qg"] + j:VOFF["qg"] + j + 1], 0.125)))(),
                     [bVEC], [bQG8])
            top = state["top"]
            H = alloc([8, NT], BF16)
            HB = [[Buf() for _ in range(5)] for _ in range(8)]
            toph = state["top"]
            norm(i, 0, H, HB)
            S.barrier()
            state["top"] = toph
            if stop_after == "norm":
                break
            if i % 2 == 0:
                even_mixer(i, H, HB)
            else:
                conformer(i, H, HB)
            S.barrier()
            state["top"] = top
            if stop_after == (i, 0) or stop_after in ("pool", "qkv", "qkvA", "qkvB", "qkvC1"):
                break
            H = alloc([8, NT], BF16)
            HB = [[Buf() for _ in range(5)] for _ in range(8)]
            toph = state["top"]
            norm(i, 1, H, HB)
            S.barrier()
            state["top"] = toph
            ffn(i, H, HB)
            S.barrier()
            state["top"] = top
            if stop_after == (i, 1):
                break

        for kc in range(8):
            DMA("sp", yT_d[:, kc * NT:(kc + 1) * NT], X[:, kc, :], dout, [XB[kc][t] for t in range(5)], [])
        S.emit(block, sems, final_waits=[dout])
    return nc


def _fm(v):
    v = np.asarray(v, np.float32)
    lead = v.shape[:-1]
    n = v.shape[-1] // 128
    v = v.reshape(lead + (n, 128))
    return np.moveaxis(v, -1, 0)


def _band_matrices():
    def dmat(S_, w):
        t = np.arange(S_)
        lo = np.clip(t - w // 2, 0, S_ - 1)
        hi = np.clip(t - w // 2 + w - 1, 0, S_ - 1)
        D = np.zeros((S_, S_), np.float64)
        for a in range(S_):
            D[a, lo[a]:hi[a] + 1] = 1.0 / (hi[a] - lo[a] + 1)
        D -= np.eye(S_)
        return D.T.astype(np.float32)
    band = np.zeros((4, 4, 128, 6, 512), np.float32)
    bandc = np.zeros((4, 128, 2, 256), np.float32)
    for g, w in enumerate((2, 4, 8, 16)):
        DT = dmat(2048, w)
        for ot in range(4):
            for rel in range(6):
                it = 4 * ot - 1 + rel
                if 0 <= it < 16:
                    band[g, ot, :, rel, :] = DT[it * 128:(it + 1) * 128, ot * 512:(ot + 1) * 512]
        DC = dmat(256, w)
        for it in range(2):
            bandc[g, :, it, :] = DC[it * 128:(it + 1) * 128, :]
    return band.reshape(4 * 4 * 128, 6 * 512), bandc.reshape(4 * 128, 512)


def _prep_shared(inp):
    f = lambda k: np.asarray(inp[k], np.float32)
    sh = {}
    w_mod = f("w_mod")
    sh["wmod"] = np.ascontiguousarray(
        w_mod.reshape(4, 8, 128, 12, 512).transpose(0, 3, 2, 1, 4)).reshape(4 * 12 * 128, 4096)
    w_in = f("w_in_ab")
    sh["winab"] = np.ascontiguousarray(
        w_in.reshape(2, 8, 128, 8, 256).transpose(0, 3, 2, 1, 4)).reshape(2 * 8 * 128, 2048)
    w_out = f("w_out_ab")
    sh["woa"] = np.ascontiguousarray(
        w_out[:, :512].reshape(2, 8, 64, 1024).transpose(0, 2, 1, 3)).reshape(2 * 64, 8192)
    sh["wop"] = np.ascontiguousarray(
        w_out[:, 512:].reshape(2, 4, 128, 1024).transpose(0, 2, 1, 3)).reshape(2 * 128, 4096)
    sh["poolw"] = np.ascontiguousarray(f("pool_w").transpose(0, 2, 1, 3)).reshape(2 * 128, 512)
    sh["band"], sh["bandc"] = _band_matrices()
    rpb = f("rpb")
    c = np.arange(64)
    cs = np.clip(c - 8, 0, 48)
    valid = (c[:, None] >= cs[None, :]) & (c[:, None] < cs[None, :] + 16)
    dcol = np.clip(c[:, None] - c[None, :], -15, 15) + 15
    drs = np.arange(7, -8, -1) + 7
    ex = rpb[:, :, drs][:, :, :, dcol]
    ex = np.where(valid[None, None, None], ex, np.float32(-1e30))
    sh["rpbx"] = np.ascontiguousarray(ex.transpose(0, 1, 3, 2, 4)).reshape(2 * 8 * 64, 960)
    pw1 = f("conv_pw1")
    a = pw1[:, :, :1024].reshape(2, 8, 128, 8, 128)
    g = pw1[:, :, 1024:].reshape(2, 8, 128, 8, 128)
    ag = np.stack([a, g], axis=4)
    sh["pw1"] = np.ascontiguousarray(ag.transpose(0, 3, 2, 1, 4, 5)).reshape(2 * 8 * 128, 2048)
    sh["pw2"] = np.ascontiguousarray(
        f("conv_pw2").reshape(2, 8, 128, 1024).transpose(0, 2, 1, 3)).reshape(2 * 128, 8192)
    dw = f("conv_dw")
    dwd = np.zeros((2, 8, 128, 31, 128), np.float32)
    idx = np.arange(128)
    dwc = dw.reshape(2, 31, 8, 128)
    for jj in range(31):
        dwd[:, :, idx, jj, idx] = dwc[:, jj]
    sh["dwd"] = dwd.reshape(2 * 8 * 128, 31 * 128)
    wi = f("ffn_w_in")
    wa = wi[:, :, :2816].reshape(4, 8, 128, 11, 2, 128)
    wg = wi[:, :, 2816:].reshape(4, 8, 128, 11, 2, 128)
    wag = np.stack([wa, wg], axis=5)
    sh["fwi"] = np.ascontiguousarray(wag.transpose(0, 3, 2, 1, 4, 5, 6)).reshape(4 * NPIECE * 128, 4096)
    wo = f("ffn_w_out")
    sh["fwo"] = np.ascontiguousarray(
        wo.reshape(4, 11, 2, 128, 1024).transpose(0, 1, 3, 2, 4)).reshape(4 * NPIECE * 128, 2048)
    cm = np.zeros((128, 4, 128), np.float32)
    cm[:, 0, :] = 1.0 / 1024.0
    cm[:64, 1, :64] = 1.0 / 64.0
    cm[64:, 1, 64:] = 1.0 / 64.0
    cm[:, 2, :] = 1.0
    cm[idx, 3, idx] = 1.0
    sh["cmat"] = cm.reshape(128, 512)
    vt = np.zeros((128, NV), np.float32)

    def put(name, arr):
        arr = np.asarray(arr, np.float32).reshape(128, -1)
        vt[:, VOFF[name]:VOFF[name] + arr.shape[1]] = arr
    put("bmod", _fm(f("b_mod")))
    put("nmix", _fm(f("norm_mix")))
    put("nffn", _fm(f("norm_ffn")))
    put("fcw", np.moveaxis(_fm(f("ffn_conv_w")), 2, 3))
    put("fcb", _fm(f("ffn_conv_b")))
    put("dwb", _fm(f("conv_dw_b")))
    put("lng", _fm(f("conv_ln_g")))
    put("lnb", _fm(f("conv_ln_b")))
    put("pscale", _fm(f("pool_scale")))
    put("qg", np.tile(f("q_gain"), (1, 2)).T)
    put("kg", np.tile(f("k_gain"), (1, 2)).T)
    sh["_vt"] = vt
    return sh


def _prep_core(inp, sh, b):
    f = lambda k: np.asarray(inp[k], np.float32)
    xs = f("x_sample")[b]
    xp = f("x_prompt")[2 * b:2 * b + 2].reshape(512, 1024)
    xa = np.concatenate([xs, xp], axis=0)
    xT = np.ascontiguousarray(xa.T.reshape(8, 128, NT).transpose(1, 0, 2)).reshape(128, 8 * NT)
    vt = sh["_vt"].copy()
    cv = np.stack([_fm(f("c")[b]), _fm(f("c_ctx"))], axis=-1)
    vt[:, VOFF["cvec"]:VOFF["cvec"] + 16] = cv.reshape(128, 16)
    ck = f("cache_k")[b]
    ckT = ck.reshape(2, 512, 4, 128).transpose(0, 3, 2, 1)
    cvv = f("cache_v")[b].reshape(2, 4, 128, 512).transpose(0, 2, 1, 3)
    m = {k: v for k, v in sh.items() if not k.startswith("_")}
    m["xT"] = xT
    m["vecs"] = vt
    m["ck"] = np.ascontiguousarray(ckT).reshape(2 * 128, 2048)
    m["cv"] = np.ascontiguousarray(cvv).reshape(2 * 128, 2048)
    return m


_NC_CACHE = {}


_NCORES = [8]


def kernel(**inputs):
    ncores = _NCORES[0]
    sh = _prep_shared(inputs)
    in_maps = [_prep_core(inputs, sh, b) for b in range(ncores)]
    if "nc" not in _NC_CACHE:
        _NC_CACHE["nc"] = build_program()
    nc = _NC_CACHE["nc"]
    res = run_bass_kernel_spmd(nc, in_maps, core_ids=list(range(ncores)))
    y_prompt = np.zeros((16, 256, 1024), np.float32)
    y_sample = np.zeros((8, 2048, 1024), np.float32)
    nk = np.zeros((16, 2, 256, 8, 64), np.float32)
    nv = np.zeros((16, 2, 256, 8, 64), np.float32)
    for b in range(ncores):
        r = res.results[b]
        yT = np.asarray(r["yT"], np.float32).reshape(128, 8, NT).transpose(2, 1, 0).reshape(NT, 1024)
        y_sample[b] = yT[:2048]
        y_prompt[2 * b:2 * b + 2] = yT[2048:].reshape(2, 256, 1024)
        k_ = np.asarray(r["nk"], np.float32).reshape(2, 128, 4, 2, 256)
        k_ = k_.transpose(3, 0, 4, 2, 1).reshape(2, 2, 256, 512)
        nk[2 * b:2 * b + 2] = k_.reshape(2, 2, 256, 8, 64)
        v_ = np.asarray(r["nv"], np.float32).reshape(2, 128, 4, 512)
        v_ = v_.transpose(0, 2, 1, 3).reshape(2, 2, 256, 512)
        nv[2 * b:2 * b + 2] = v_.transpose(1, 0, 2, 3).reshape(2, 2, 256, 8, 64)
    return (y_prompt, y_sample, nk, nv)
```

```python
import numpy as np
import concourse.bass as bass
import concourse.mybir as mybir
from concourse.bass_utils import run_bass_kernel_spmd

F32 = mybir.dt.float32
BF16 = mybir.dt.bfloat16
AF = mybir.ActivationFunctionType
ALU = mybir.AluOpType

ENGS = ("pe", "act", "dve", "pool", "sp")
NT = 2560
EPS = 1e-6
TL = [(0, 512, 0), (512, 512, 0), (1024, 512, 0), (1536, 512, 0), (2048, 512, 1)]
NPIECE = 11
DD = 12
NBLK = 30


class Buf:
    __slots__ = ("writer", "readers", "excl")

    def __init__(self, excl=False):
        self.writer = None
        self.readers = []
        self.excl = excl


class DmaSem:
    __slots__ = ("sem", "count")

    def __init__(self, sem):
        self.sem = sem
        self.count = 0


class Op:
    __slots__ = ("eng", "fn", "deps", "dsem", "dval", "signal", "sigval", "is_dma", "rawdeps")


class Sched:
    def __init__(self):
        self.ops = {e: [] for e in ENGS}
        self.dsems = []
        self.bar_ops = []
        self.bar_raw = []

    def new_dsem(self, sem):
        d = DmaSem(sem)
        self.dsems.append(d)
        return d

    def op(self, eng, fn, reads=(), writes=()):
        o = Op()
        o.eng = eng
        o.fn = fn
        o.is_dma = False
        o.dsem = None
        o.dval = 0
        o.signal = False
        o.sigval = 0
        deps = list(self.bar_ops)
        o.rawdeps = self.bar_raw
        for r in reads:
            if r.writer is not None:
                deps.append(r.writer)
            if r.excl:
                deps.extend(x for x in r.readers if x.eng != eng)
        for w in writes:
            if w.writer is not None:
                deps.append(w.writer)
            deps.extend(w.readers)
        for w in writes:
            w.writer = o
            w.readers = []
        for r in reads:
            r.readers.append(o)
        o.deps = deps
        self.ops[eng].append(o)
        return o

    def dma(self, eng, fn, dsem, reads=(), writes=()):
        o = self.op(eng, fn, reads, writes)
        o.is_dma = True
        dsem.count += 16
        o.dsem = dsem
        o.dval = dsem.count
        return o

    def barrier(self):
        ops = []
        for e in ENGS:
            for o in reversed(self.ops[e]):
                if not o.is_dma:
                    ops.append(o)
                    break
        self.bar_ops = ops
        self.bar_raw = [(d, d.count) for d in self.dsems if d.count > 0]

    def emit(self, block, sems, final_waits=()):
        for e in ENGS:
            for o in self.ops[e]:
                for d in o.deps:
                    if d.is_dma:
                        continue
                    if d.eng == o.eng and d.eng in ("pe", "sp"):
                        continue
                    d.signal = True
        for e in ENGS:
            c = 0
            for o in self.ops[e]:
                if o.is_dma:
                    continue
                if o.signal:
                    c += 1
                    o.sigval = c

        def run(e, eng):
            known = {}
            for o in self.ops[e]:
                need = {}
                for d in o.deps:
                    if d.is_dma:
                        key = ("d", id(d.dsem))
                        sem, val = d.dsem.sem, d.dval
                    else:
                        if d.eng == e and e in ("pe", "sp"):
                            continue
                        key = ("c", d.eng)
                        sem, val = sems[d.eng], d.sigval
                    if key not in need or need[key][1] < val:
                        need[key] = (sem, val)
                for (ds, val) in o.rawdeps:
                    key = ("d", id(ds))
                    if key not in need or need[key][1] < val:
                        need[key] = (ds.sem, val)
                for key, (sem, val) in need.items():
                    if known.get(key, 0) >= val:
                        continue
                    eng.wait_ge(sem, val)
                    known[key] = val
                ins = o.fn(eng)
                if o.is_dma:
                    ins.then_inc(o.dsem.sem, 16)
                elif o.signal:
                    ins.then_inc(sems[e], 1)
            if e == "sp":
                for ds in final_waits:
                    if ds.count > 0:
                        eng.wait_ge(ds.sem, ds.count)

        @block.tensor
        def _(eng):
            run("pe", eng)

        @block.scalar
        def _(eng):
            run("act", eng)

        @block.vector
        def _(eng):
            run("dve", eng)

        @block.gpsimd
        def _(eng):
            run("pool", eng)

        @block.sync
        def _(eng):
            run("sp", eng)


def _vec_layout():
    off = {}
    n = 0
    for name, sz in (("cvec", 16), ("bmod", 4 * 48), ("nmix", 32), ("nffn", 32),
                     ("fcw", 4 * 44 * 3), ("fcb", 4 * 44), ("dwb", 16), ("lng", 16),
                     ("lnb", 16), ("pscale", 8), ("qg", 2), ("kg", 2)):
        off[name] = n
        n += sz
    return off, n


VOFF, NV = _vec_layout()


def build_program(stop_after=None):
    nc = bass.Bass("TRN2", target_bir_lowering=False)

    def din(name, shape):
        return nc.dram_tensor(name, list(shape), F32, kind="ExternalInput").ap()

    xT_d = din("xT", [128, 8 * NT])
    vecs_d = din("vecs", [128, NV])
    cmat_d = din("cmat", [128, 4 * 128])
    wmod_d = din("wmod", [4 * 12 * 128, 4096])
    winab_d = din("winab", [2 * 8 * 128, 2048])
    woa_d = din("woa", [2 * 64, 8192])
    wop_d = din("wop", [2 * 128, 4096])
    poolw_d = din("poolw", [2 * 128, 512])
    band_d = din("band", [4 * 4 * 128, 6 * 512])
    bandc_d = din("bandc", [4 * 128, 2 * 256])
    rpbx_d = din("rpbx", [2 * 8 * 64, 960])
    ck_d = din("ck", [2 * 128, 2048])
    cv_d = din("cv", [2 * 128, 2048])
    pw1_d = din("pw1", [2 * 8 * 128, 2048])
    pw2_d = din("pw2", [2 * 128, 8192])
    dwd_d = din("dwd", [2 * 8 * 128, 31 * 128])
    fwi_d = din("fwi", [4 * NPIECE * 128, 4096])
    fwo_d = din("fwo", [4 * NPIECE * 128, 2048])

    yT_d = nc.dram_tensor("yT", [128, 8 * NT], F32, kind="ExternalOutput").ap()
    nk_d = nc.dram_tensor("nk", [2 * 128, 2048], F32, kind="ExternalOutput").ap()
    nv_d = nc.dram_tensor("nv", [2 * 128, 2048], F32, kind="ExternalOutput").ap()
    qs_d = nc.dram_tensor("qscr", [128, 4 * NT], BF16, kind="ExternalOutput").ap()
    tab_d = nc.dram_tensor("tabscr", [2 * 8 * 128, 2 * NBLK * 64], BF16, kind="ExternalOutput").ap()

    ARENA_F32 = 53200
    arena = nc.alloc_sbuf_tensor("arena", [128, ARENA_F32], F32)
    psum = nc.alloc_psum_tensor("psum", [128, 4096], F32)

    S = Sched()
    state = {"top": 0, "bank": 0, "skip": None}

    def alloc(shape, dtype):
        n = 1
        for s in shape:
            n *= s
        nb = n * (2 if dtype == BF16 else 4)
        nb = (nb + 63) // 64 * 64
        sk = state["skip"]
        if sk is not None and state["top"] < sk[1] and state["top"] + nb > sk[0]:
            state["top"] = sk[1]
        o4 = state["top"] // 4
        state["top"] += nb
        assert state["top"] <= ARENA_F32 * 4, ("SBUF arena overflow", state["top"])
        v = arena[:, o4:o4 + nb // 4]
        if dtype == BF16:
            v = v.bitcast(BF16)
        v = v[:, 0:n]
        if len(shape) == 2:
            v = v.rearrange("p (a b) -> p a b", b=shape[1])
        elif len(shape) == 3:
            v = v.rearrange("p (a b c) -> p a b c", b=shape[1], c=shape[2])
        elif len(shape) == 4:
            v = v.rearrange("p (a b c d) -> p a b c d", b=shape[1], c=shape[2], d=shape[3])
        return v

    PB = [Buf(excl=True) for _ in range(8)]

    reserved = set()

    def bank(reserve=False):
        i = state["bank"]
        while i in reserved:
            i = (i + 1) % 8
        state["bank"] = (i + 1) % 8
        if reserve:
            reserved.add(i)
        return psum[:, i * 512:(i + 1) * 512], PB[i]

    gstate = {}

    def gbank(name, ids):
        k = gstate.get(name, 0)
        gstate[name] = k + 1
        i = ids[k % len(ids)]
        return psum[:, i * 512:(i + 1) * 512], PB[i]

    def MM(out, lhsT, rhs, start, stop, reads, writes):
        S.op("pe", lambda e: e.matmul(out, lhsT, rhs, start=start, stop=stop), reads, writes)

    def ACT(out, in_, func, reads, writes, bias=None, scale=None):
        kw = {}
        if bias is not None:
            kw["bias"] = bias
        if scale is not None:
            kw["scale"] = scale
        S.op("act", lambda e: e.activation(out=out, in_=in_, func=func, **kw), reads, writes)

    def TT(eng, out, in0, in1, op, reads, writes):
        S.op(eng, lambda e: e.tensor_tensor(out=out, in0=in0, in1=in1, op=op), reads, writes)

    def STT(eng, out, in0, scalar, in1, op0, op1, reads, writes):
        S.op(eng, lambda e: e.scalar_tensor_tensor(out=out, in0=in0, scalar=scalar, in1=in1,
                                                   op0=op0, op1=op1), reads, writes)

    def RECIP(out, in_, reads, writes):
        S.op("dve", lambda e: e.reciprocal(out=out, in_=in_), reads, writes)

    def MEMSET(eng, ap, val, writes):
        S.op(eng, lambda e: e.memset(ap, val), (), writes)

    def DMA(eng, out, in_, dsem, reads, writes):
        S.dma(eng, lambda e: e.dma_start(out=out, in_=in_), dsem, reads, writes)

    from contextlib import ExitStack
    es = ExitStack()
    with es:
        E = es.enter_context
        sems = {e: E(nc.semaphore("s_" + e)) for e in ENGS}
        dpool = [S.new_dsem(E(nc.semaphore("d%d" % i))) for i in range(40)]
        dstate = {"i": 0}

        def dsem():
            d = dpool[dstate["i"] % len(dpool)]
            dstate["i"] += 1
            return d

        dout = S.new_dsem(E(nc.semaphore("dout")))
        block = E(nc.Block())

        X = alloc([8, NT], F32)
        XB = [[Buf() for _ in range(5)] for _ in range(8)]
        VEC = alloc([NV], F32)
        CM = alloc([4, 128], BF16)
        ONESF = alloc([64], F32)
        MOD = alloc([4, 6, 8, 2], F32)
        AV = alloc([2, 8, 2], F32)
        QG8 = alloc([1], F32)
        bVEC, bCM, bAV, bONESF, bQG8 = Buf(), Buf(), Buf(), Buf(), Buf()
        bMODL = [Buf() for _ in range(4)]
        ST = alloc([8, 2], BF16)
        bST = Buf()
        PERSIST_TOP = state["top"]

        def vec(name, *idx):
            o = VOFF[name]
            dims = {"cvec": (8, 2), "bmod": (4, 48), "nmix": (4, 8), "nffn": (4, 8),
                    "fcw": (4, 44, 3), "fcb": (4, 44), "dwb": (2, 8), "lng": (2, 8),
                    "lnb": (2, 8), "pscale": (2, 4), "qg": (2,), "kg": (2,)}[name]
            lin = 0
            for d, i_ in zip(dims, idx):
                lin = lin * d + i_
            return VEC[:, o + lin:o + lin + 1]

        ONES1024 = CM[:, 0, :]
        BLK64 = CM[:, 1, :]

        d0 = dsem()
        for kc in range(8):
            DMA("sp", X[:, kc, :], xT_d[:, kc * NT:(kc + 1) * NT], d0, (), [XB[kc][t] for t in range(5)])
        d1 = dsem()
        DMA("sp", VEC, vecs_d, d1, (), [bVEC])
        d2 = dsem()
        DMA("pool", CM, cmat_d.rearrange("p (a b) -> p a b", b=128), d2, (), [bCM])
        MEMSET("dve", ONESF, 1.0, [bONESF])

        cv_ap = VEC[:, VOFF["cvec"]:VOFF["cvec"] + 16].rearrange("p (a b) -> p a b", b=2)
        ACT(ST, cv_ap, AF.Silu, [bVEC], [bST])
        def mod_plan(i, WMl, bWMl, dWMl):
            pm, bpm = bank(reserve=True)
            bi = (state["bank"] - 1) % 8
            pmv = pm[:, 0:96].rearrange("p (a b) -> p a b", b=2)
            nr = len(WMl)

            def dma(pc):
                r = pc % nr
                row = (i * 12 + pc) * 128
                DMA("pool", WMl[r], wmod_d[row:row + 128, :].rearrange("p (a b) -> p a b", b=512),
                    dWMl[r], (), [bWMl[r]])

            def mms(pc):
                r = pc % nr
                for q in range(4):
                    cc = pc * 4 + q
                    for kc in range(8):
                        MM(pmv[:, cc, :], WMl[r][:, kc, q * 128:(q + 1) * 128], ST[:, kc, :],
                           kc == 0, kc == 7, [bWMl[r], bST], [bpm])

            def fin():
                bm = VEC[:, VOFF["bmod"] + i * 48:VOFF["bmod"] + (i + 1) * 48]
                for s_ in range(2):
                    TT("dve", MOD[:, i, :, :, s_], pmv[:, :, s_].rearrange("p (a b) -> p a b", b=8),
                       bm.rearrange("p (a b) -> p a b", b=8), ALU.add, [bpm, bVEC], [bMODL[i]])
                reserved.discard(bi)
            return dma, mms, fin

        XSp = [alloc([15, 64], F32) for _ in range(2)]
        bXSp = [Buf(), Buf()]
        dXSp = [dsem(), dsem()]
        TBp = [alloc([2, NBLK, 64], BF16) for _ in range(2)]
        bTBp = [Buf(), Buf()]
        dTABo = [dsem(), dsem()]
        bTABd = [[Buf() for _ in range(8)] for _ in range(2)]
        for r_ in range(2):
            MEMSET("dve", TBp[r_], 0.0, [bTBp[r_]])
        def xs_load(u_):
            rx = u_ % 2
            row = u_ * 64
            for half in range(2):
                DMA("sp", XSp[rx][half * 64:(half + 1) * 64],
                    rpbx_d[row:row + 64, :].rearrange("p (a b) -> p a b", b=64), dXSp[rx], (), [bXSp[rx]])

        xs_load(0)
        for u_ in range(16):
            j_, h_ = u_ // 8, u_ % 8
            rx = u_ % 2
            if u_ + 1 < 16:
                xs_load(u_ + 1)
            ACT(TBp[rx][0:64, 0, DD - 7:DD + 8, :], XSp[rx][0:64], AF.Exp, [bXSp[rx]], [bTBp[rx]])
            ACT(TBp[rx][64:128, 0, DD - 6:DD + 9, :], XSp[rx][64:128], AF.Exp, [bXSp[rx]], [bTBp[rx]])
            ACT(TBp[rx][0:64, 1, DD - 3:DD + 5, :], XSp[rx][0:64, 4:12, :], AF.Exp, [bXSp[rx]], [bTBp[rx]])
            ACT(TBp[rx][64:128, 1, DD - 2:DD + 6, :], XSp[rx][64:128, 4:12, :], AF.Exp, [bXSp[rx]], [bTBp[rx]])
            DMA("act", tab_d[u_ * 128:(u_ + 1) * 128, :], TBp[rx].rearrange("p a b c -> p (a b c)"), dTABo[rx],
                [bTBp[rx]], [bTABd[j_][h_]])
        WM = [alloc([8, 512], BF16) for _ in range(2)]
        bWM = [Buf(), Buf()]
        dWM = [dsem(), dsem()]
        dma0, mms0, fin0 = mod_plan(0, WM, bWM, dWM)
        dma0(0)
        for pc in range(12):
            if pc + 1 < 12:
                dma0(pc + 1)
            mms0(pc)
        fin0()
        S.barrier()
        state["top"] = PERSIST_TOP

        def modv(i, m, kc, s):
            return MOD[:, i, m, kc, s:s + 1]

        def layer_vectors(i):
            nm = VEC[:, VOFF["nmix"] + i * 8:VOFF["nmix"] + (i + 1) * 8]
            nf = VEC[:, VOFF["nffn"] + i * 8:VOFF["nffn"] + (i + 1) * 8]
            for s in range(2):
                STT("dve", AV[:, 0, :, s], MOD[:, i, 1, :, s], 1.0, nm, ALU.add, ALU.mult,
                    [bMODL[i], bVEC], [bAV])
                STT("dve", AV[:, 1, :, s], MOD[:, i, 4, :, s], 1.0, nf, ALU.add, ALU.mult,
                    [bMODL[i], bVEC], [bAV])

        def norm(i, which, H, HB):
            SQ = [alloc([8, 512], BF16) for _ in range(2)]
            bSQ = [Buf(), Buf()]
            SD = [alloc([512], F32) for _ in range(2)]
            RS = [alloc([512], F32) for _ in range(2)]
            bSD = [Buf(), Buf()]
            bRS = [Buf(), Buf()]
            TMP = [alloc([512], F32) for _ in range(4)]
            bTMP = [Buf() for _ in range(4)]
            bm = 0 if which == 0 else 3
            k = 0
            for ti, (t0, n, s) in enumerate(TL):
                r = ti % 2
                for kc in range(8):
                    ACT(SQ[r][:, kc, :], X[:, kc, t0:t0 + n], AF.Square, [XB[kc][ti]], [bSQ[r]])
                pa, bpa = bank()
                for kc in range(8):
                    MM(pa, ONES1024, SQ[r][:, kc, :], kc == 0, kc == 7, [bSQ[r], bCM], [bpa])
                ACT(SD[r], pa, AF.Ln, [bpa, bEPS], [bSD[r]], bias=EPSV, scale=1.0)
                ACT(RS[r], SD[r], AF.Exp, [bSD[r]], [bRS[r]], scale=-0.5)
                for kc in range(8):
                    q = k % 4
                    k += 1
                    TT("dve", TMP[q], X[:, kc, t0:t0 + n], RS[r], ALU.mult, [XB[kc][ti], bRS[r]], [bTMP[q]])
                    ACT(H[:, kc, t0:t0 + n], TMP[q], AF.Identity, [bTMP[q], bAV, bMODL[i]], [HB[kc][ti]],
                        bias=modv(i, bm, kc, s), scale=AV[:, which, kc, s:s + 1])

        def resid_add(pso, bpso, i, gm, oc, ti):
            t0, n, s = TL[ti]
            STT("dve", X[:, oc, t0:t0 + n], pso, modv(i, gm, oc, s), X[:, oc, t0:t0 + n],
                ALU.mult, ALU.add, [bpso, bMODL[i]], [XB[oc][ti]])

        EPSV = alloc([1], F32)
        bEPS = Buf()
        MEMSET("dve", EPSV, EPS, [bEPS])
        PERSIST_TOP = state["top"]
        S.barrier()

        def ffn(i, H, HB):
            WI = [alloc([8, 512], BF16) for _ in range(2)]
            WO = [alloc([2, 1024], BF16) for _ in range(4)]
            bWI = [Buf(), Buf()]
            bWO = [Buf() for _ in range(4)]
            dWI = [dsem(), dsem()]
            dWO = [dsem() for _ in range(4)]
            ACTB = [alloc([2, NT], BF16) for _ in range(3)]
            bACT = [[[Buf() for _ in range(5)] for _ in range(2)] for _ in range(3)]
            NY = 5
            YB = [alloc([512], F32) for _ in range(NY)]
            bY = [Buf() for _ in range(NY)]
            yk = {"k": 0}

            def ybuf():
                q = yk["k"] % NY
                yk["k"] += 1
                return YB[q], bY[q]

            def load(pc):
                row = (i * NPIECE + pc) * 128
                DMA("pool", WI[pc % 2], fwi_d[row:row + 128, :].rearrange("p (a b) -> p a b", b=512),
                    dWI[pc % 2], (), [bWI[pc % 2]])
                DMA("pool", WO[pc % 4], fwo_d[row:row + 128, :].rearrange("p (a b) -> p a b", b=1024),
                    dWO[pc % 4], (), [bWO[pc % 4]])

            def up(pc):
                r = pc % 2
                ra = pc % 3
                for f in range(2):
                    fc = pc * 2 + f
                    prev = [None, None]
                    ys = {}

                    def finish(ti):
                        t0, n, s = TL[ti]
                        ya, bya = ys.pop((0, ti))
                        yg, byg = ys.pop((1, ti))
                        ACT(ya, ya, AF.Silu, [bya], [bya])
                        TT("pool", ACTB[ra][:, f, t0:t0 + n], ya, yg, ALU.mult, [bya, byg], [bACT[ra][f][ti]])

                    for ti, (t0, n, s) in enumerate(TL):
                        for br in range(2):
                            ch = fc + 22 * br
                            w0, w1, w2 = (vec("fcw", i, ch, 0), vec("fcw", i, ch, 1), vec("fcw", i, ch, 2))
                            bb = vec("fcb", i, ch)
                            col = (f * 2 + br) * 128
                            pu, bpu = bank()
                            for kc in range(8):
                                MM(pu, WI[r][:, kc, col:col + 128], H[:, kc, t0:t0 + n], kc == 0, kc == 7,
                                   [bWI[r], HB[kc][ti]], [bpu])
                            y, by = ybuf()
                            ys[(br, ti)] = (y, by)
                            ACT(y, pu, AF.Identity, [bpu, bVEC], [by], bias=bb, scale=w1)
                            if s == 0:
                                rngs = [(0, 512)]
                            else:
                                rngs = [(0, 256), (256, 512)]
                            for (a, b) in rngs:
                                STT("dve", y[:, a + 1:b], pu[:, a:b - 1], w0, y[:, a + 1:b], ALU.mult, ALU.add,
                                    [bpu, by, bVEC], [by])
                                STT("dve", y[:, a:b - 1], pu[:, a + 1:b], w2, y[:, a:b - 1], ALU.mult, ALU.add,
                                    [bpu, by, bVEC], [by])
                            if s == 0 and ti > 0:
                                ppu, bppu, py, bpy = prev[br]
                                STT("dve", y[:, 0:1], ppu[:, 511:512], w0, y[:, 0:1], ALU.mult, ALU.add,
                                    [bppu, by, bVEC], [by])
                                STT("dve", py[:, 511:512], pu[:, 0:1], w2, py[:, 511:512], ALU.mult, ALU.add,
                                    [bpu, bpy, bVEC], [bpy])
                            prev[br] = (pu, bpu, y, by)
                        if s == 0 and ti > 0:
                            finish(ti - 1)
                        if (s == 0 and ti == 3) or s == 1:
                            finish(ti)

            def down(pcs):
                nmm = 2 * len(pcs)
                for ti, (t0, n, s) in enumerate(TL):
                    for oc in range(8):
                        po, bpo = bank()
                        k_ = 0
                        for pc in pcs:
                            for f in range(2):
                                MM(po, WO[pc % 4][:, f, oc * 128:(oc + 1) * 128], ACTB[pc % 3][:, f, t0:t0 + n],
                                   k_ == 0, k_ == nmm - 1, [bWO[pc % 4], bACT[pc % 3][f][ti]], [bpo])
                                k_ += 1
                        resid_add(po, bpo, i, 5, oc, ti)

            if i + 1 < 4:
                WMf = [alloc([8, 512], BF16)]
                mdma, mmms, mfin = mod_plan(i + 1, WMf, [Buf()], [dsem()])
                mdma(0)
            load(0)
            for pc in range(NPIECE):
                if pc + 1 < NPIECE:
                    load(pc + 1)
                if i + 1 < 4:
                    mmms(pc)
                    mdma(pc + 1)
                    if pc == NPIECE - 1:
                        mmms(pc + 1)
                up(pc)
                if pc >= 2 and pc % 2 == 0:
                    down([pc - 2, pc - 1])
            down([NPIECE - 1])
            if i + 1 < 4:
                mfin()

        LG = 2650
        GB = [15, 2093, 2379]

        def conformer(i, H, HB):
            j = i // 2
            top0 = state["top"]
            GLU = alloc([8, LG], BF16)
            bGLU = [Buf() for _ in range(8)]
            for c in range(8):
                MEMSET("dve", GLU[:, c, :], 0.0, [bGLU[c]])
            W1 = [alloc([8, 256], BF16) for _ in range(2)]
            bW1 = [Buf(), Buf()]
            dW1 = [dsem(), dsem()]
            SG = [alloc([512], F32) for _ in range(3)]
            bSG = [Buf() for _ in range(3)]
            k = 0
            for oc in range(8):
                r = oc % 2
                row = (j * 8 + oc) * 128
                DMA("pool", W1[r], pw1_d[row:row + 128, :].rearrange("p (a b) -> p a b", b=256),
                    dW1[r], (), [bW1[r]])
                for ti, (t0, n, s) in enumerate(TL):
                    pa, bpa = bank()
                    pg, bpg = bank()
                    for kc in range(8):
                        MM(pa, W1[r][:, kc, 0:128], H[:, kc, t0:t0 + n], kc == 0, kc == 7, [bW1[r], HB[kc][ti]], [bpa])
                    for kc in range(8):
                        MM(pg, W1[r][:, kc, 128:256], H[:, kc, t0:t0 + n], kc == 0, kc == 7, [bW1[r], HB[kc][ti]], [bpg])
                    q = k % 3
                    k += 1
                    ACT(SG[q], pg, AF.Sigmoid, [bpg], [bSG[q]])
                    if s == 0:
                        TT("dve", GLU[:, oc, GB[0] + t0:GB[0] + t0 + n], pa, SG[q], ALU.mult, [bpa, bSG[q]], [bGLU[oc]])
                    else:
                        for c2 in range(2):
                            TT("dve", GLU[:, oc, GB[1 + c2]:GB[1 + c2] + 256], pa[:, c2 * 256:(c2 + 1) * 256],
                               SG[q][:, c2 * 256:(c2 + 1) * 256], ALU.mult, [bpa, bSG[q]], [bGLU[oc]])
            S.barrier()
            state["top"] = top0
            GLU2 = alloc([8, LG], BF16)
            VB = H
            bVB = [[Buf() for _ in range(5)] for _ in range(8)]
            DG = [alloc([31, 128], BF16) for _ in range(2)]
            bDG = [Buf(), Buf()]
            dDG = [dsem(), dsem()]
            for c in range(8):
                r = c % 2
                row = (j * 8 + c) * 128
                DMA("pool", DG[r], dwd_d[row:row + 128, :].rearrange("p (a b) -> p a b", b=128),
                    dDG[r], (), [bDG[r]])
                for ti, (t0, n, s) in enumerate(TL):
                    pv, bpv = bank()
                    if s == 0:
                        for jj in range(31):
                            st = GB[0] + t0 + jj - 15
                            MM(pv, DG[r][:, jj, :], GLU2[:, c, st:st + 512], jj == 0, jj == 30, [bDG[r], bGLU[c]], [bpv])
                    else:
                        for c2 in range(2):
                            for jj in range(31):
                                st = GB[1 + c2] + jj - 15
                                MM(pv[:, c2 * 256:(c2 + 1) * 256], DG[r][:, jj, :], GLU2[:, c, st:st + 256],
                                   jj == 0, jj == 30, [bDG[r], bGLU[c]], [bpv])
                    ACT(VB[:, c, t0:t0 + n], pv, AF.Identity, [bpv, bVEC], [bVB[c][ti]], bias=vec("dwb", j, c), scale=1.0)
            S.barrier()
            state["top"] = top0
            W2 = alloc([8, 1024], BF16)
            bW2 = Buf()
            dW2 = dsem()
            DMA("pool", W2, pw2_d[j * 128:(j + 1) * 128, :].rearrange("p (a b) -> p a b", b=1024), dW2, (), [bW2])
            SQ = [alloc([8, 512], BF16) for _ in range(2)]
            bSQ = [Buf(), Buf()]
            SS = [alloc([8, 512], BF16) for _ in range(2)]
            bSS = [[Buf() for _ in range(8)] for _ in range(2)]
            MS = [alloc([512], F32) for _ in range(2)]
            bMS = [Buf(), Buf()]
            M2 = [alloc([512], F32) for _ in range(2)]
            bM2 = [Buf(), Buf()]
            RS = [alloc([512], F32) for _ in range(2)]
            bRS = [Buf(), Buf()]
            T1 = [alloc([512], F32) for _ in range(4)]
            bT1 = [Buf() for _ in range(4)]
            k = 0
            for ti, (t0, n, s) in enumerate(TL):
                r = ti % 2
                for c in range(8):
                    ACT(SQ[r][:, c, :], VB[:, c, t0:t0 + n], AF.Square, [bVB[c][ti]], [bSQ[r]])
                pm, bpm = bank()
                pq, bpq = bank()
                for c in range(8):
                    MM(pm, ONES1024, VB[:, c, t0:t0 + n], c == 0, c == 7, [bVB[c][ti], bCM], [bpm])
                for c in range(8):
                    MM(pq, ONES1024, SQ[r][:, c, :], c == 0, c == 7, [bSQ[r], bCM], [bpq])
                ACT(MS[r], pm, AF.Identity, [bpm], [bMS[r]])
                TT("dve", M2[r], MS[r], MS[r], ALU.mult, [bMS[r]], [bM2[r]])
                TT("dve", M2[r], pq, M2[r], ALU.subtract, [bpq, bM2[r]], [bM2[r]])
                ACT(M2[r], M2[r], AF.Ln, [bM2[r], bEPS], [bM2[r]], bias=EPSV, scale=1.0)
                ACT(RS[r], M2[r], AF.Exp, [bM2[r]], [bRS[r]], scale=-0.5)
                for c in range(8):
                    q = k % 4
                    k += 1
                    TT("dve", T1[q], VB[:, c, t0:t0 + n], MS[r], ALU.subtract, [bVB[c][ti], bMS[r]], [bT1[q]])
                    TT("dve", T1[q], T1[q], RS[r], ALU.mult, [bT1[q], bRS[r]], [bT1[q]])
                    ACT(SS[r][:, c, :], T1[q], AF.Silu, [bT1[q], bVEC], [bSS[r][c]],
                        bias=vec("lnb", j, c), scale=vec("lng", j, c))
                for oc in range(8):
                    po, bpo = bank()
                    for c in range(8):
                        MM(po, W2[:, c, oc * 128:(oc + 1) * 128], SS[r][:, c, :], c == 0, c == 7, [bW2, bSS[r][c]], [bpo])
                    resid_add(po, bpo, i, 2, oc, ti)
            S.barrier()
            state["top"] = top0

        def even_mixer(i, H, HB):
            j = i // 2
            top0 = state["top"]
            WP = alloc([8, 512], BF16)
            bWP = Buf()
            dWP = dsem()
            for pc in range(2):
                row = (j * 8 + 6 + pc) * 128
                DMA("pool", WP[:, :, pc * 256:(pc + 1) * 256],
                    winab_d[row:row + 128, :].rearrange("p (a b) -> p a b", b=256), dWP, (), [bWP])
            PW = alloc([4, 128], BF16)
            WOP = alloc([4, 1024], BF16)
            bPW, bWOP = Buf(), Buf()
            DMA("pool", PW, poolw_d[j * 128:(j + 1) * 128, :].rearrange("p (a b) -> p a b", b=128), dsem(), (), [bPW])
            DMA("pool", WOP, wop_d[j * 128:(j + 1) * 128, :].rearrange("p (a b) -> p a b", b=1024), dsem(), (), [bWOP])
            PTM = alloc([20, 512], BF16)
            bPTM = [Buf() for _ in range(20)]
            for tt in range(20):
                ti = tt // 4
                pp, bpp = bank()
                for kc in range(8):
                    MM(pp, H[:, kc, tt * 128:(tt + 1) * 128], WP[:, kc, :], kc == 0, kc == 7, [HB[kc][ti], bWP], [bpp])
                ACT(PTM[:, tt, :], pp, AF.Identity, [bpp], [bPTM[tt]])
            BND = [alloc([6, 512], BF16) for _ in range(3)]
            bBND = [Buf(), Buf(), Buf()]
            dBND = [dsem(), dsem(), dsem()]
            DT = [alloc([512], BF16) for _ in range(2)]
            bDT = [Buf(), Buf()]
            YP = [alloc([4, 512], BF16) for _ in range(2)]
            bYP = [[Buf() for _ in range(4)] for _ in range(2)]
            k = 0
            for ti, (t0, n, s) in enumerate(TL):
                ry = ti % 2
                for g in range(4):
                    r = k % 3
                    k += 1
                    pd, bpd = bank()
                    if s == 0:
                        row = (g * 4 + ti) * 128
                        DMA("pool", BND[r], band_d[row:row + 128, :].rearrange("p (a b) -> p a b", b=512),
                            dBND[r], (), [bBND[r]])
                        its = [it for it in range(4 * ti - 1, 4 * ti + 5) if 0 <= it < 16]
                        for n_, it in enumerate(its):
                            MM(pd, PTM[:, it, g * 128:(g + 1) * 128], BND[r][:, it - (4 * ti - 1), :],
                               n_ == 0, n_ == len(its) - 1, [bPTM[it], bBND[r]], [bpd])
                    else:
                        DMA("pool", BND[r][:, 0:2, 0:256],
                            bandc_d[g * 128:(g + 1) * 128, :].rearrange("p (a b) -> p a b", b=256),
                            dBND[r], (), [bBND[r]])
                        for c2 in range(2):
                            for it2 in range(2):
                                it = 16 + 2 * c2 + it2
                                MM(pd[:, c2 * 256:(c2 + 1) * 256], PTM[:, it, g * 128:(g + 1) * 128],
                                   BND[r][:, it2, 0:256], it2 == 0, it2 == 1, [bPTM[it], bBND[r]], [bpd])
                    rd = k % 2
                    ACT(DT[rd], pd, AF.Identity, [bpd], [bDT[rd]])
                    py, bpy = bank()
                    MM(py, PW[:, g, :], DT[rd], True, True, [bPW, bDT[rd]], [bpy])
                    ACT(YP[ry][:, g, :], py, AF.Identity, [bpy, bVEC], [bYP[ry][g]], scale=vec("pscale", j, g))
                for oc in range(8):
                    po, bpo = bank()
                    for g in range(4):
                        MM(po, WOP[:, g, oc * 128:(oc + 1) * 128], YP[ry][:, g, :], g == 0, g == 3,
                           [bWOP, bYP[ry][g]], [bpo])
                    resid_add(po, bpo, i, 2, oc, ti)
            S.barrier()
            state["top"] = top0
            if stop_after == "pool":
                return
            h_lo = top0 - 8 * NT * 2
            KT = alloc([4, NT], BF16)
            bKT = [[Buf() for _ in range(5)] for _ in range(4)]
            VT = alloc([24, 8, 66], BF16)
            bVT = [Buf() for _ in range(24)]
            CK = alloc([4, 512], BF16)
            bCK = Buf()
            top_keep = state["top"]
            MEMSET("dve", VT[:, :, :, 64:66], 1.0, [bVT[tt] for tt in range(24)])
            DMA("pool", CK, ck_d[j * 128:(j + 1) * 128, :].rearrange("p (a b) -> p a b", b=512), dsem(), (), [bCK])
            dcv = dsem()
            for t in range(4):
                DMA("pool", VT[:, 20 + t, :, 0:64],
                    cv_d[j * 128:(j + 1) * 128, t * 512:(t + 1) * 512].rearrange("p (a b) -> p a b", b=64),
                    dcv, (), [bVT[20 + t]])
            if stop_after == "qkvA":
                S.barrier()
                return
            WQ = [alloc([8, 256], BF16) for _ in range(2)]
            bWQ = [Buf(), Buf()]
            dWQ = [dsem(), dsem()]
            SQ1 = [alloc([512], BF16) for _ in range(2)]
            bSQ1 = [Buf(), Buf()]
            SD = [alloc([512], F32) for _ in range(2)]
            bSD = [Buf(), Buf()]
            RS = [alloc([512], F32) for _ in range(2)]
            bRS = [Buf(), Buf()]
            QST = [alloc([512], BF16) for _ in range(3)]
            bQST = [Buf() for _ in range(3)]
            KOUT = [alloc([512], F32) for _ in range(2)]
            bKOUT = [Buf(), Buf()]
            bQS = [[Buf() for _ in range(5)] for _ in range(4)]
            dQS = dsem()
            k = 0
            kq = 0
            ko = 0
            for pc in range(4):
                r = pc % 2
                row = (j * 8 + pc) * 128
                DMA("pool", WQ[r], winab_d[row:row + 128, :].rearrange("p (a b) -> p a b", b=256),
                    dWQ[r], (), [bWQ[r]])
                for ti, (t0, n, s) in enumerate(TL):
                    for c2 in range(2):
                        ch = (pc % 2) * 2 + c2
                        pq, bpq = bank()
                        for kc in range(8):
                            MM(pq, WQ[r][:, kc, c2 * 128:(c2 + 1) * 128], H[:, kc, t0:t0 + n], kc == 0, kc == 7,
                               [bWQ[r], HB[kc][ti]], [bpq])
                        q = k % 2
                        k += 1
                        ACT(SQ1[q], pq, AF.Square, [bpq], [bSQ1[q]])
                        pn, bpn = bank()
                        MM(pn, BLK64, SQ1[q], True, True, [bCM, bSQ1[q]], [bpn])
                        ACT(SD[q], pn, AF.Ln, [bpn, bEPS], [bSD[q]], bias=EPSV, scale=1.0)
                        ACT(RS[q], SD[q], AF.Exp, [bSD[q]], [bRS[q]], scale=-0.5)
                        if pc < 2:
                            qq = kq % 3
                            kq += 1
                            STT("dve", QST[qq], pq, QG8, RS[q], ALU.mult, ALU.mult, [bpq, bQG8, bRS[q]], [bQST[qq]])
                            DMA("sp", qs_d[:, ch * NT + t0:ch * NT + t0 + n], QST[qq], dQS, [bQST[qq]], [bQS[ch][ti]])
                        else:
                            kgv = VEC[:, VOFF["kg"] + j:VOFF["kg"] + j + 1]
                            STT("dve", KT[:, ch, t0:t0 + n], pq, kgv, RS[q], ALU.mult, ALU.mult,
                                [bpq, bVEC, bRS[q]], [bKT[ch][ti]])
                            if s == 1:
                                o_ = ko % 2
                                ko += 1
                                STT("dve", KOUT[o_], pq, kgv, RS[q], ALU.mult, ALU.mult,
                                    [bpq, bVEC, bRS[q]], [bKOUT[o_]])
                                DMA("sp", nk_d[j * 128:(j + 1) * 128, ch * 512:(ch + 1) * 512], KOUT[o_], dout,
                                    [bKOUT[o_]], [])
            if stop_after == "qkvB":
                S.barrier()
                return
            for pc in range(2):
                row = (j * 8 + 4 + pc) * 128
                DMA("pool", WQ[pc], winab_d[row:row + 128, :].rearrange("p (a b) -> p a b", b=256),
                    dWQ[pc], (), [bWQ[pc]])
            VOUT = KOUT
            bVOUT = bKOUT
            for tt in range(20):
                ti = tt // 4
                pv, bpv = bank()
                for pc in range(2):
                    for kc in range(8):
                        MM(pv[:, pc * 256:(pc + 1) * 256], H[:, kc, tt * 128:(tt + 1) * 128], WQ[pc][:, kc, :],
                           kc == 0, kc == 7, [HB[kc][ti], bWQ[pc]], [bpv])
                if stop_after != "qkvC1":
                    for hh in range(8):
                        ACT(VT[:, tt, hh, 0:64], pv[:, hh * 64:(hh + 1) * 64], AF.Identity, [bpv], [bVT[tt]])
                if tt >= 16:
                    o_ = tt % 2
                    S.op("dve", (lambda o_=o_, pv=pv: (lambda e: e.tensor_copy(out=VOUT[o_], in_=pv)))(),
                         [bpv], [bVOUT[o_]])
                    DMA("sp", nv_d[j * 128:(j + 1) * 128, (tt - 16) * 512:(tt - 15) * 512], VOUT[o_], dout,
                        [bVOUT[o_]], [])
            S.barrier()
            if stop_after in ("qkv", "qkvC1"):
                return
            state["top"] = h_lo
            state["skip"] = (top0, top_keep)
            WOA = alloc([8, 1024], BF16)
            bWOA = Buf()
            MEMSET("dve", WOA[64:128], 0.0, [bWOA])
            DMA("pool", WOA[0:64], woa_d[j * 64:(j + 1) * 64, :].rearrange("p (a b) -> p a b", b=1024),
                dsem(), (), [bWOA])
            AT0 = alloc([8, 512], BF16)
            bAT0 = [Buf() for _ in range(8)]
            MEMSET("dve", AT0, 0.0, bAT0)
            AT = [AT0, AT0]
            bAT = [bAT0, bAT0]
            QZ = [alloc([8, 512], BF16) for _ in range(2)]
            bQT = [Buf(), Buf()]
            dQT = [dsem(), dsem()]
            for r_ in range(2):
                MEMSET("dve", QZ[r_], 0.0, [bQT[r_]])
            TBL = [alloc([2, NBLK, 64], BF16) for _ in range(2)]
            bTB = [Buf(), Buf()]
            dTB = [dsem(), dsem()]
            NE = 8
            EB = [alloc([512], BF16) for _ in range(NE)]
            bEB = [Buf() for _ in range(NE)]
            OS = [alloc([512], F32) for _ in range(2)]
            bOS = [Buf(), Buf()]
            RC = [alloc([512], F32) for _ in range(2)]
            bRC = [Buf(), Buf()]
            ek = {"k": 0, "hk": 0, "mk": 0}

            def ebuf():
                q = ek["k"] % NE
                ek["k"] += 1
                return EB[q], bEB[q]

            def meng():
                ek["mk"] += 1
                return "dve"

            def finalize(po, bpo, ncol, at, bat, h, c0):
                q = ek["hk"] % 2
                ek["hk"] += 1
                ACT(OS[q][0:65, 0:ncol], po[0:65, 0:ncol], AF.Identity, [bpo], [bOS[q]])
                RECIP(RC[q][64:65, 0:ncol], OS[q][64:65, 0:ncol], [bOS[q]], [bRC[q]])
                pb, bpb = bank()
                MM(pb[0:64, 0:ncol], ONESF[64:65, 0:64], RC[q][64:65, 0:ncol], True, True, [bONESF, bRC[q]], [bpb])
                TT("dve", at[0:64, h, c0:c0 + ncol], OS[q][0:64, 0:ncol], pb[0:64, 0:ncol], ALU.mult,
                   [bOS[q], bpb], [bat[h]])

            from collections import deque
            LOOK = 5
            DEFER = 5
            units = []
            for ti, (t0, n, s) in enumerate(TL):
                for h in range(8):
                    if s == 0:
                        units.append((ti, h, None))
                    else:
                        units.append((ti, h, "c"))
            steps = []
            for ui, (ti, h, c2) in enumerate(units):
                if c2 is None:
                    r0 = 8 * ti
                    rs0 = min(max(r0 - 4, 0), 24)
                    rs7 = min(max(r0 + 7 - 4, 0), 24)
                    tl = [("cache", t) for t in range(4)] + [("local", kr0) for kr0 in range(rs0, rs7 + 8, 2)]
                else:
                    tl = [("ctx", (cq, t)) for cq in range(2) for t in range(2)]
                for k_, d_ in enumerate(tl):
                    steps.append((ui, d_, k_ == 0, k_ == len(tl) - 1))
            local_units = [ui for ui, u in enumerate(units) if u[2] is None]
            recs = {}
            ust = {}
            pend = deque()
            reserved.clear()

            def load_q(ti):
                t0, n, s = TL[ti]
                rq = ti % 2
                qv = QZ[rq].rearrange("p (c two) n -> p c two n", two=2)
                for half in range(2):
                    src = qs_d[half * 64:(half + 1) * 64, :].rearrange("p (c t) -> p c t", t=NT)
                    DMA("sp", qv[half * 64:(half + 1) * 64, :, half, :], src[:, :, t0:t0 + n], dQT[rq],
                        [bQS[ch][ti] for ch in range(4)], [bQT[rq]])

            def load_tab(ui):
                ti, h, c2 = units[ui]
                rx = local_units.index(ui) % 2
                DMA("sp", TBL[rx].rearrange("p a b c -> p (a b c)"),
                    tab_d[(j * 8 + h) * 128:(j * 8 + h + 1) * 128, :], dTB[rx], [bTABd[j][h]], [bTB[rx]])

            utab = {}

            def unit_start(ui):
                ti, h, c2 = units[ui]
                if h == 0:
                    if ti == 0:
                        load_q(0)
                    if ti + 1 < len(TL):
                        load_q(ti + 1)
                if c2 is None:
                    li = local_units.index(ui)
                    if li == 0:
                        load_tab(ui)
                    if li + 1 < len(local_units):
                        load_tab(local_units[li + 1])
                    utab[ui] = li % 2

            def emit_S(idx):
                ui, (kind, arg), first, last = steps[idx]
                ti, h, c2 = units[ui]
                rq = ti % 2
                c = h // 2
                p0 = (h % 2) * 64
                psS, bps = bank()
                eb, beb = ebuf()
                if kind == "ctx":
                    cq, t = arg
                    tt = 16 + 2 * cq + t
                    MM(psS[:, 0:256], KT[:, c, tt * 128:(tt + 1) * 128],
                       QZ[rq][:, h, cq * 256:(cq + 1) * 256], True, True, [bKT[c][4], bQT[rq]], [bps])
                    ACT(eb[:, 0:256], psS[:, 0:256], AF.Exp, [bps], [beb])
                    recs[idx] = (eb[:, 0:256], beb, tt, 256, cq * 256, t == 0, t == 1)
                    return
                if kind == "cache":
                    t = arg
                    MM(psS, CK[:, c, t * 128:(t + 1) * 128], QZ[rq][:, h, :], True, True,
                       [bCK, bQT[rq]], [bps])
                    ACT(eb, psS, AF.Exp, [bps], [beb])
                    recs[idx] = (eb, beb, 20 + t, 512, 0, None, None)
                    return
                kr0 = arg
                r0 = 8 * ti
                rt = utab[ui]
                TE_, TI_, btb = TBL[rt][:, 0], TBL[rt][:, 1], bTB[rt]
                MM(psS, KT[:, c, kr0 * 64:kr0 * 64 + 128], QZ[rq][:, h, :], True, True,
                   [bKT[c][kr0 // 8], bQT[rq]], [bps])
                ACT(eb, psS, AF.Exp, [bps], [beb])

                def tslice(tab, qa, nrow):
                    j0 = DD - (kr0 - qa)
                    return tab[:, j0:j0 + nrow, :]

                def ev(a_, b_):
                    return eb[:, a_ * 64:b_ * 64].rearrange("p (a b) -> p a b", b=64)
                if ti in (1, 2):
                    TT("dve", ev(0, 8), ev(0, 8), tslice(TI_, r0, 8), ALU.mult, [beb, btb], [beb])
                elif ti == 0:
                    TT("dve", ev(4, 8), ev(4, 8), tslice(TI_, 4, 4), ALU.mult, [beb, btb], [beb])
                    if kr0 < 8:
                        TT("dve", ev(0, 4), ev(0, 4), tslice(TE_, 0, 4), ALU.mult, [beb, btb], [beb])
                    else:
                        MEMSET("dve", eb[:, 0:256], 0.0, [beb])
                else:
                    TT("dve", ev(0, 5), ev(0, 5), tslice(TI_, 24, 5), ALU.mult, [beb, btb], [beb])
                    if kr0 >= 24:
                        TT("dve", ev(5, 8), ev(5, 8), tslice(TE_, 29, 3), ALU.mult, [beb, btb], [beb])
                    else:
                        MEMSET("dve", eb[:, 320:512], 0.0, [beb])
                recs[idx] = (eb, beb, kr0 // 2, 512, 0, None, None)

            def out_proj(ti):
                rq = ti % 2
                at, bat = AT[rq], bAT[rq]
                for oc in range(8):
                    po2, bpo2 = bank()
                    for h in range(8):
                        MM(po2, WOA[:, h, oc * 128:(oc + 1) * 128], at[:, h, :], h == 0, h == 7,
                           [bWOA, bat[h]], [bpo2])
                    resid_add(po2, bpo2, i, 2, oc, ti)

            def emit_PV(idx, now):
                ui, (kind, arg), first, last = steps[idx]
                ti, h, c2 = units[ui]
                eb, beb, tt, ncol, cofs, st_, sp_ = recs.pop(idx)
                st_ = first if st_ is None else st_
                sp_ = last if sp_ is None else sp_
                if first:
                    i_ = state["bank"]
                    po, bpo = bank(reserve=True)
                    ust[ui] = (po, bpo, (state["bank"] - 1) % 8)
                po, bpo, bi = ust[ui]
                vflat = VT[:, tt, :, :].rearrange("p a b -> p (a b)")
                mw = min(128, 8 * 66 - h * 66)
                MM(po[0:mw, cofs:cofs + ncol], vflat[:, h * 66:h * 66 + mw], eb, st_, sp_, [bVT[tt], beb], [bpo])
                if last:
                    q = ek["hk"] % 2
                    ek["hk"] += 1
                    rq = ti % 2
                    at, bat = AT[rq], bAT[rq]
                    c0 = 0
                    ncol = 512
                    ACT(OS[q][0:65, 0:ncol], po[0:65, 0:ncol], AF.Identity, [bpo], [bOS[q]])
                    ACT(RC[q][64:65, 0:ncol], OS[q][64:65, 0:ncol], AF.Ln, [bOS[q]], [bRC[q]])
                    ACT(RC[q][64:65, 0:ncol], RC[q][64:65, 0:ncol], AF.Exp, [bRC[q]], [bRC[q]], scale=-1.0)

                    def stage_b(q=q, ncol=ncol, at=at, bat=bat, h=h, c0=c0, bi=bi, ti=ti, c2=c2, ui=ui):
                        pb, bpb = bank()
                        MM(pb[0:64, 0:ncol], ONESF[64:65, 0:64], RC[q][64:65, 0:ncol], True, True,
                           [bONESF, bRC[q]], [bpb])
                        TT("dve", at[0:64, h, c0:c0 + ncol], OS[q][0:64, 0:ncol], pb[0:64, 0:ncol], ALU.mult,
                           [bOS[q], bpb], [bat[h]])
                        reserved.discard(bi)
                        del ust[ui]
                        if h == 7:
                            out_proj(ti)
                    pend.append((now + (DEFER if c2 is None else 3), stage_b))

            nst = len(steps)
            for idx in range(nst + LOOK + DEFER + 2):
                while pend and pend[0][0] <= idx:
                    pend.popleft()[1]()
                if idx < nst:
                    if steps[idx][2]:
                        unit_start(steps[idx][0])
                    emit_S(idx)
                jn = idx - LOOK
                if 0 <= jn < nst and steps[jn][2] and units[steps[jn][0]][2] == "c" and units[steps[jn][0]][1] == 0:
                    while pend:
                        pend.popleft()[1]()
                jdx = idx - LOOK
                if 0 <= jdx < nst:
                    emit_PV(jdx, idx)
            assert not pend and not ust and not recs
            reserved.clear()
            S.barrier()
            state["skip"] = None
            state["top"] = top0

        done = False
        for i in range(4):
            if done or stop_after == "mod":
                break
            layer_vectors(i)
            if i % 2 == 0:
                j = i // 2
                S.op("act", (lambda j=j: (lambda e: e.mul(QG8, VEC[:, VOFF["qg"] + j:VOFF["qg"] + j + 1], 0.125)))(),
                     [bVEC], [bQG8])
            top = state["top"]
            H = alloc([8, NT], BF16)
            HB = [[Buf() for _ in range(5)] for _ in range(8)]
            toph = state["top"]
            norm(i, 0, H, HB)
            S.barrier()
            state["top"] = toph
            if stop_after == "norm":
                break
            if i % 2 == 0:
                even_mixer(i, H, HB)
            else:
                conformer(i, H, HB)
            S.barrier()
            state["top"] = top
            if stop_after == (i, 0) or stop_after in ("pool", "qkv", "qkvA", "qkvB", "qkvC1"):
                break
            H = alloc([8, NT], BF16)
            HB = [[Buf() for _ in range(5)] for _ in range(8)]
            toph = state["top"]
            norm(i, 1, H, HB)
            S.barrier()
            state["top"] = toph
            ffn(i, H, HB)
            S.barrier()
            state["top"] = top
            if stop_after == (i, 1):
                break

        for kc in range(8):
            DMA("sp", yT_d[:, kc * NT:(kc + 1) * NT], X[:, kc, :], dout, [XB[kc][t] for t in range(5)], [])
        S.emit(block, sems, final_waits=[dout])
    return nc


def _fm(v):
    v = np.asarray(v, np.float32)
    lead = v.shape[:-1]
    n = v.shape[-1] // 128
    v = v.reshape(lead + (n, 128))
    return np.moveaxis(v, -1, 0)


def _band_matrices():
    def dmat(S_, w):
        t = np.arange(S_)
        lo = np.clip(t - w // 2, 0, S_ - 1)
        hi = np.clip(t - w // 2 + w - 1, 0, S_ - 1)
        D = np.zeros((S_, S_), np.float64)
        for a in range(S_):
            D[a, lo[a]:hi[a] + 1] = 1.0 / (hi[a] - lo[a] + 1)
        D -= np.eye(S_)
        return D.T.astype(np.float32)
    band = np.zeros((4, 4, 128, 6, 512), np.float32)
    bandc = np.zeros((4, 128, 2, 256), np.float32)
    for g, w in enumerate((2, 4, 8, 16)):
        DT = dmat(2048, w)
        for ot in range(4):
            for rel in range(6):
                it = 4 * ot - 1 + rel
                if 0 <= it < 16:
                    band[g, ot, :, rel, :] = DT[it * 128:(it + 1) * 128, ot * 512:(ot + 1) * 512]
        DC = dmat(256, w)
        for it in range(2):
            bandc[g, :, it, :] = DC[it * 128:(it + 1) * 128, :]
    return band.reshape(4 * 4 * 128, 6 * 512), bandc.reshape(4 * 128, 512)


def _prep_shared(inp):
    f = lambda k: np.asarray(inp[k], np.float32)
    sh = {}
    w_mod = f("w_mod")
    sh["wmod"] = np.ascontiguousarray(
        w_mod.reshape(4, 8, 128, 12, 512).transpose(0, 3, 2, 1, 4)).reshape(4 * 12 * 128, 4096)
    w_in = f("w_in_ab")
    sh["winab"] = np.ascontiguousarray(
        w_in.reshape(2, 8, 128, 8, 256).transpose(0, 3, 2, 1, 4)).reshape(2 * 8 * 128, 2048)
    w_out = f("w_out_ab")
    sh["woa"] = np.ascontiguousarray(
        w_out[:, :512].reshape(2, 8, 64, 1024).transpose(0, 2, 1, 3)).reshape(2 * 64, 8192)
    sh["wop"] = np.ascontiguousarray(
        w_out[:, 512:].reshape(2, 4, 128, 1024).transpose(0, 2, 1, 3)).reshape(2 * 128, 4096)
    sh["poolw"] = np.ascontiguousarray(f("pool_w").transpose(0, 2, 1, 3)).reshape(2 * 128, 512)
    sh["band"], sh["bandc"] = _band_matrices()
    rpb = f("rpb")
    c = np.arange(64)
    cs = np.clip(c - 8, 0, 48)
    valid = (c[:, None] >= cs[None, :]) & (c[:, None] < cs[None, :] + 16)
    dcol = np.clip(c[:, None] - c[None, :], -15, 15) + 15
    drs = np.arange(7, -8, -1) + 7
    ex = rpb[:, :, drs][:, :, :, dcol]
    ex = np.where(valid[None, None, None], ex, np.float32(-1e30))
    sh["rpbx"] = np.ascontiguousarray(ex.transpose(0, 1, 3, 2, 4)).reshape(2 * 8 * 64, 960)
    pw1 = f("conv_pw1")
    a = pw1[:, :, :1024].reshape(2, 8, 128, 8, 128)
    g = pw1[:, :, 1024:].reshape(2, 8, 128, 8, 128)
    ag = np.stack([a, g], axis=4)
    sh["pw1"] = np.ascontiguousarray(ag.transpose(0, 3, 2, 1, 4, 5)).reshape(2 * 8 * 128, 2048)
    sh["pw2"] = np.ascontiguousarray(
        f("conv_pw2").reshape(2, 8, 128, 1024).transpose(0, 2, 1, 3)).reshape(2 * 128, 8192)
    dw = f("conv_dw")
    dwd = np.zeros((2, 8, 128, 31, 128), np.float32)
    idx = np.arange(128)
    dwc = dw.reshape(2, 31, 8, 128)
    for jj in range(31):
        dwd[:, :, idx, jj, idx] = dwc[:, jj]
    sh["dwd"] = dwd.reshape(2 * 8 * 128, 31 * 128)
    wi = f("ffn_w_in")
    wa = wi[:, :, :2816].reshape(4, 8, 128, 11, 2, 128)
    wg = wi[:, :, 2816:].reshape(4, 8, 128, 11, 2, 128)
    wag = np.stack([wa, wg], axis=5)
    sh["fwi"] = np.ascontiguousarray(wag.transpose(0, 3, 2, 1, 4, 5, 6)).reshape(4 * NPIECE * 128, 4096)
    wo = f("ffn_w_out")
    sh["fwo"] = np.ascontiguousarray(
        wo.reshape(4, 11, 2, 128, 1024).transpose(0, 1, 3, 2, 4)).reshape(4 * NPIECE * 128, 2048)
    cm = np.zeros((128, 4, 128), np.float32)
    cm[:, 0, :] = 1.0 / 1024.0
    cm[:64, 1, :64] = 1.0 / 64.0
    cm[64:, 1, 64:] = 1.0 / 64.0
    cm[:, 2, :] = 1.0
    cm[idx, 3, idx] = 1.0
    sh["cmat"] = cm.reshape(128, 512)
    vt = np.zeros((128, NV), np.float32)

    def put(name, arr):
        arr = np.asarray(arr, np.float32).reshape(128, -1)
        vt[:, VOFF[name]:VOFF[name] + arr.shape[1]] = arr
    put("bmod", _fm(f("b_mod")))
    put("nmix", _fm(f("norm_mix")))
    put("nffn", _fm(f("norm_ffn")))
    put("fcw", np.moveaxis(_fm(f("ffn_conv_w")), 2, 3))
    put("fcb", _fm(f("ffn_conv_b")))
    put("dwb", _fm(f("conv_dw_b")))
    put("lng", _fm(f("conv_ln_g")))
    put("lnb", _fm(f("conv_ln_b")))
    put("pscale", _fm(f("pool_scale")))
    put("qg", np.tile(f("q_gain"), (1, 2)).T)
    put("kg", np.tile(f("k_gain"), (1, 2)).T)
    sh["_vt"] = vt
    return sh


def _prep_core(inp, sh, b):
    f = lambda k: np.asarray(inp[k], np.float32)
    xs = f("x_sample")[b]
    xp = f("x_prompt")[2 * b:2 * b + 2].reshape(512, 1024)
    xa = np.concatenate([xs, xp], axis=0)
    xT = np.ascontiguousarray(xa.T.reshape(8, 128, NT).transpose(1, 0, 2)).reshape(128, 8 * NT)
    vt = sh["_vt"].copy()
    cv = np.stack([_fm(f("c")[b]), _fm(f("c_ctx"))], axis=-1)
    vt[:, VOFF["cvec"]:VOFF["cvec"] + 16] = cv.reshape(128, 16)
    ck = f("cache_k")[b]
    ckT = ck.reshape(2, 512, 4, 128).transpose(0, 3, 2, 1)
    cvv = f("cache_v")[b].reshape(2, 4, 128, 512).transpose(0, 2, 1, 3)
    m = {k: v for k, v in sh.items() if not k.startswith("_")}
    m["xT"] = xT
    m["vecs"] = vt
    m["ck"] = np.ascontiguousarray(ckT).reshape(2 * 128, 2048)
    m["cv"] = np.ascontiguousarray(cvv).reshape(2 * 128, 2048)
    return m


_NC_CACHE = {}


_NCORES = [8]


def kernel(**inputs):
    ncores = _NCORES[0]
    sh = _prep_shared(inputs)
    in_maps = [_prep_core(inputs, sh, b) for b in range(ncores)]
    if "nc" not in _NC_CACHE:
        _NC_CACHE["nc"] = build_program()
    nc = _NC_CACHE["nc"]
    res = run_bass_kernel_spmd(nc, in_maps, core_ids=list(range(ncores)))
    y_prompt = np.zeros((16, 256, 1024), np.float32)
    y_sample = np.zeros((8, 2048, 1024), np.float32)
    nk = np.zeros((16, 2, 256, 8, 64), np.float32)
    nv = np.zeros((16, 2, 256, 8, 64), np.float32)
    for b in range(ncores):
        r = res.results[b]
        yT = np.asarray(r["yT"], np.float32).reshape(128, 8, NT).transpose(2, 1, 0).reshape(NT, 1024)
        y_sample[b] = yT[:2048]
        y_prompt[2 * b:2 * b + 2] = yT[2048:].reshape(2, 256, 1024)
        k_ = np.asarray(r["nk"], np.float32).reshape(2, 128, 4, 2, 256)
        k_ = k_.transpose(3, 0, 4, 2, 1).reshape(2, 2, 256, 512)
        nk[2 * b:2 * b + 2] = k_.reshape(2, 2, 256, 8, 64)
        v_ = np.asarray(r["nv"], np.float32).reshape(2, 128, 4, 512)
        v_ = v_.transpose(0, 2, 1, 3).reshape(2, 2, 256, 512)
        nv[2 * b:2 * b + 2] = v_.transpose(1, 0, 2, 3).reshape(2, 2, 256, 8, 64)
    return (y_prompt, y_sample, nk, nv)
```

```python
import numpy as np
import concourse.bass as bass
import concourse.mybir as mybir
from concourse.bass_utils import run_bass_kernel_spmd

F32 = mybir.dt.float32
BF16 = mybir.dt.bfloat16
AF = mybir.ActivationFunctionType
ALU = mybir.AluOpType

ENGS = ("pe", "act", "dve", "pool", "sp")
NT = 2560
EPS = 1e-6
TL = [(0, 512, 0), (512, 512, 0), (1024, 512, 0), (1536, 512, 0), (2048, 512, 1)]
NPIECE = 11
DD = 12
NBLK = 30


class Buf:
    __slots__ = ("writer", "readers", "excl")

    def __init__(self, excl=False):
        self.writer = None
        self.readers = []
        self.excl = excl


class DmaSem:
    __slots__ = ("sem", "count")

    def __init__(self, sem):
        self.sem = sem
        self.count = 0


class Op:
    __slots__ = ("eng", "fn", "deps", "dsem", "dval", "signal", "sigval", "is_dma", "rawdeps")


class Sched:
    def __init__(self):
        self.ops = {e: [] for e in ENGS}
        self.dsems = []
        self.bar_ops = []
        self.bar_raw = []

    def new_dsem(self, sem):
        d = DmaSem(sem)
        self.dsems.append(d)
        return d

    def op(self, eng, fn, reads=(), writes=()):
        o = Op()
        o.eng = eng
        o.fn = fn
        o.is_dma = False
        o.dsem = None
        o.dval = 0
        o.signal = False
        o.sigval = 0
        deps = list(self.bar_ops)
        o.rawdeps = self.bar_raw
        for r in reads:
            if r.writer is not None:
                deps.append(r.writer)
            if r.excl:
                deps.extend(x for x in r.readers if x.eng != eng)
        for w in writes:
            if w.writer is not None:
                deps.append(w.writer)
            deps.extend(w.readers)
        for w in writes:
            w.writer = o
            w.readers = []
        for r in reads:
            r.readers.append(o)
        o.deps = deps
        self.ops[eng].append(o)
        return o

    def dma(self, eng, fn, dsem, reads=(), writes=()):
        o = self.op(eng, fn, reads, writes)
        o.is_dma = True
        dsem.count += 16
        o.dsem = dsem
        o.dval = dsem.count
        return o

    def barrier(self):
        ops = []
        for e in ENGS:
            for o in reversed(self.ops[e]):
                if not o.is_dma:
                    ops.append(o)
                    break
        self.bar_ops = ops
        self.bar_raw = [(d, d.count) for d in self.dsems if d.count > 0]

    def emit(self, block, sems, final_waits=()):
        for e in ENGS:
            for o in self.ops[e]:
                for d in o.deps:
                    if d.is_dma:
                        continue
                    if d.eng == o.eng and d.eng in ("pe", "sp"):
                        continue
                    d.signal = True
        for e in ENGS:
            c = 0
            for o in self.ops[e]:
                if o.is_dma:
                    continue
                if o.signal:
                    c += 1
                    o.sigval = c

        def run(e, eng):
            known = {}
            for o in self.ops[e]:
                need = {}
                for d in o.deps:
                    if d.is_dma:
                        key = ("d", id(d.dsem))
                        sem, val = d.dsem.sem, d.dval
                    else:
                        if d.eng == e and e in ("pe", "sp"):
                            continue
                        key = ("c", d.eng)
                        sem, val = sems[d.eng], d.sigval
                    if key not in need or need[key][1] < val:
                        need[key] = (sem, val)
                for (ds, val) in o.rawdeps:
                    key = ("d", id(ds))
                    if key not in need or need[key][1] < val:
                        need[key] = (ds.sem, val)
                for key, (sem, val) in need.items():
                    if known.get(key, 0) >= val:
                        continue
                    eng.wait_ge(sem, val)
                    known[key] = val
                ins = o.fn(eng)
                if o.is_dma:
                    ins.then_inc(o.dsem.sem, 16)
                elif o.signal:
                    ins.then_inc(sems[e], 1)
            if e == "sp":
                for ds in final_waits:
                    if ds.count > 0:
                        eng.wait_ge(ds.sem, ds.count)

        @block.tensor
        def _(eng):
            run("pe", eng)

        @block.scalar
        def _(eng):
            run("act", eng)

        @block.vector
        def _(eng):
            run("dve", eng)

        @block.gpsimd
        def _(eng):
            run("pool", eng)

        @block.sync
        def _(eng):
            run("sp", eng)


def _vec_layout():
    off = {}
    n = 0
    for name, sz in (("cvec", 16), ("bmod", 4 * 48), ("nmix", 32), ("nffn", 32),
                     ("fcw", 4 * 44 * 3), ("fcb", 4 * 44), ("dwb", 16), ("lng", 16),
                     ("lnb", 16), ("pscale", 8), ("qg", 2), ("kg", 2)):
        off[name] = n
        n += sz
    return off, n


VOFF, NV = _vec_layout()


def build_program(stop_after=None):
    nc = bass.Bass("TRN2", target_bir_lowering=False)

    def din(name, shape):
        return nc.dram_tensor(name, list(shape), F32, kind="ExternalInput").ap()

    xT_d = din("xT", [128, 8 * NT])
    vecs_d = din("vecs", [128, NV])
    cmat_d = din("cmat", [128, 4 * 128])
    wmod_d = din("wmod", [4 * 12 * 128, 4096])
    winab_d = din("winab", [2 * 8 * 128, 2048])
    woa_d = din("woa", [2 * 64, 8192])
    wop_d = din("wop", [2 * 128, 4096])
    poolw_d = din("poolw", [2 * 128, 512])
    band_d = din("band", [4 * 4 * 128, 6 * 512])
    bandc_d = din("bandc", [4 * 128, 2 * 256])
    rpbx_d = din("rpbx", [2 * 8 * 64, 960])
    ck_d = din("ck", [2 * 128, 2048])
    cv_d = din("cv", [2 * 128, 2048])
    pw1_d = din("pw1", [2 * 8 * 128, 2048])
    pw2_d = din("pw2", [2 * 128, 8192])
    dwd_d = din("dwd", [2 * 8 * 128, 31 * 128])
    fwi_d = din("fwi", [4 * NPIECE * 128, 4096])
    fwo_d = din("fwo", [4 * NPIECE * 128, 2048])

    yT_d = nc.dram_tensor("yT", [128, 8 * NT], F32, kind="ExternalOutput").ap()
    nk_d = nc.dram_tensor("nk", [2 * 128, 2048], F32, kind="ExternalOutput").ap()
    nv_d = nc.dram_tensor("nv", [2 * 128, 2048], F32, kind="ExternalOutput").ap()
    qs_d = nc.dram_tensor("qscr", [128, 4 * NT], BF16, kind="ExternalOutput").ap()
    tab_d = nc.dram_tensor("tabscr", [2 * 8 * 128, 2 * NBLK * 64], BF16, kind="ExternalOutput").ap()

    ARENA_F32 = 53200
    arena = nc.alloc_sbuf_tensor("arena", [128, ARENA_F32], F32)
    psum = nc.alloc_psum_tensor("psum", [128, 4096], F32)

    S = Sched()
    state = {"top": 0, "bank": 0, "skip": None}

    def alloc(shape, dtype):
        n = 1
        for s in shape:
            n *= s
        nb = n * (2 if dtype == BF16 else 4)
        nb = (nb + 63) // 64 * 64
        sk = state["skip"]
        if sk is not None and state["top"] < sk[1] and state["top"] + nb > sk[0]:
            state["top"] = sk[1]
        o4 = state["top"] // 4
        state["top"] += nb
        assert state["top"] <= ARENA_F32 * 4, ("SBUF arena overflow", state["top"])
        v = arena[:, o4:o4 + nb // 4]
        if dtype == BF16:
            v = v.bitcast(BF16)
        v = v[:, 0:n]
        if len(shape) == 2:
            v = v.rearrange("p (a b) -> p a b", b=shape[1])
        elif len(shape) == 3:
            v = v.rearrange("p (a b c) -> p a b c", b=shape[1], c=shape[2])
        elif len(shape) == 4:
            v = v.rearrange("p (a b c d) -> p a b c d", b=shape[1], c=shape[2], d=shape[3])
        return v

    PB = [Buf(excl=True) for _ in range(8)]

    reserved = set()

    def bank(reserve=False):
        i = state["bank"]
        while i in reserved:
            i = (i + 1) % 8
        state["bank"] = (i + 1) % 8
        if reserve:
            reserved.add(i)
        return psum[:, i * 512:(i + 1) * 512], PB[i]

    gstate = {}

    def gbank(name, ids):
        k = gstate.get(name, 0)
        gstate[name] = k + 1
        i = ids[k % len(ids)]
        return psum[:, i * 512:(i + 1) * 512], PB[i]

    def MM(out, lhsT, rhs, start, stop, reads, writes):
        S.op("pe", lambda e: e.matmul(out, lhsT, rhs, start=start, stop=stop), reads, writes)

    def ACT(out, in_, func, reads, writes, bias=None, scale=None):
        kw = {}
        if bias is not None:
            kw["bias"] = bias
        if scale is not None:
            kw["scale"] = scale
        S.op("act", lambda e: e.activation(out=out, in_=in_, func=func, **kw), reads, writes)

    def TT(eng, out, in0, in1, op, reads, writes):
        S.op(eng, lambda e: e.tensor_tensor(out=out, in0=in0, in1=in1, op=op), reads, writes)

    def STT(eng, out, in0, scalar, in1, op0, op1, reads, writes):
        S.op(eng, lambda e: e.scalar_tensor_tensor(out=out, in0=in0, scalar=scalar, in1=in1,
                                                   op0=op0, op1=op1), reads, writes)

    def RECIP(out, in_, reads, writes):
        S.op("dve", lambda e: e.reciprocal(out=out, in_=in_), reads, writes)

    def MEMSET(eng, ap, val, writes):
        S.op(eng, lambda e: e.memset(ap, val), (), writes)

    def DMA(eng, out, in_, dsem, reads, writes):
        S.dma(eng, lambda e: e.dma_start(out=out, in_=in_), dsem, reads, writes)

    from contextlib import ExitStack
    es = ExitStack()
    with es:
        E = es.enter_context
        sems = {e: E(nc.semaphore("s_" + e)) for e in ENGS}
        dpool = [S.new_dsem(E(nc.semaphore("d%d" % i))) for i in range(40)]
        dstate = {"i": 0}

        def dsem():
            d = dpool[dstate["i"] % len(dpool)]
            dstate["i"] += 1
            return d

        dout = S.new_dsem(E(nc.semaphore("dout")))
        block = E(nc.Block())

        X = alloc([8, NT], F32)
        XB = [[Buf() for _ in range(5)] for _ in range(8)]
        VEC = alloc([NV], F32)
        CM = alloc([4, 128], BF16)
        ONESF = alloc([64], F32)
        MOD = alloc([4, 6, 8, 2], F32)
        AV = alloc([2, 8, 2], F32)
        QG8 = alloc([1], F32)
        bVEC, bCM, bAV, bONESF, bQG8 = Buf(), Buf(), Buf(), Buf(), Buf()
        bMODL = [Buf() for _ in range(4)]
        ST = alloc([8, 2], BF16)
        bST = Buf()
        PERSIST_TOP = state["top"]

        def vec(name, *idx):
            o = VOFF[name]
            dims = {"cvec": (8, 2), "bmod": (4, 48), "nmix": (4, 8), "nffn": (4, 8),
                    "fcw": (4, 44, 3), "fcb": (4, 44), "dwb": (2, 8), "lng": (2, 8),
                    "lnb": (2, 8), "pscale": (2, 4), "qg": (2,), "kg": (2,)}[name]
            lin = 0
            for d, i_ in zip(dims, idx):
                lin = lin * d + i_
            return VEC[:, o + lin:o + lin + 1]

        ONES1024 = CM[:, 0, :]
        BLK64 = CM[:, 1, :]

        d0 = dsem()
        for kc in range(8):
            DMA("sp", X[:, kc, :], xT_d[:, kc * NT:(kc + 1) * NT], d0, (), [XB[kc][t] for t in range(5)])
        d1 = dsem()
        DMA("sp", VEC, vecs_d, d1, (), [bVEC])
        d2 = dsem()
        DMA("pool", CM, cmat_d.rearrange("p (a b) -> p a b", b=128), d2, (), [bCM])
        MEMSET("dve", ONESF, 1.0, [bONESF])

        cv_ap = VEC[:, VOFF["cvec"]:VOFF["cvec"] + 16].rearrange("p (a b) -> p a b", b=2)
        ACT(ST, cv_ap, AF.Silu, [bVEC], [bST])
        def mod_plan(i, WMl, bWMl, dWMl):
            pm, bpm = bank(reserve=True)
            bi = (state["bank"] - 1) % 8
            pmv = pm[:, 0:96].rearrange("p (a b) -> p a b", b=2)
            nr = len(WMl)

            def dma(pc):
                r = pc % nr
                row = (i * 12 + pc) * 128
                DMA("pool", WMl[r], wmod_d[row:row + 128, :].rearrange("p (a b) -> p a b", b=512),
                    dWMl[r], (), [bWMl[r]])

            def mms(pc):
                r = pc % nr
                for q in range(4):
                    cc = pc * 4 + q
                    for kc in range(8):
                        MM(pmv[:, cc, :], WMl[r][:, kc, q * 128:(q + 1) * 128], ST[:, kc, :],
                           kc == 0, kc == 7, [bWMl[r], bST], [bpm])

            def fin():
                bm = VEC[:, VOFF["bmod"] + i * 48:VOFF["bmod"] + (i + 1) * 48]
                for s_ in range(2):
                    TT("dve", MOD[:, i, :, :, s_], pmv[:, :, s_].rearrange("p (a b) -> p a b", b=8),
                       bm.rearrange("p (a b) -> p a b", b=8), ALU.add, [bpm, bVEC], [bMODL[i]])
                reserved.discard(bi)
            return dma, mms, fin

        bTABd = [[Buf() for _ in range(8)] for _ in range(2)]

        def build_tables(j_):
            XSp = [alloc([15, 64], F32) for _ in range(2)]
            bXSp = [Buf(), Buf()]
            dXSp = [dsem(), dsem()]
            TBp = [alloc([2, NBLK, 64], BF16) for _ in range(2)]
            bTBp = [Buf(), Buf()]
            dTABo = [dsem(), dsem()]
            for r_ in range(2):
                MEMSET("dve", TBp[r_], 0.0, [bTBp[r_]])

            def xs_load(h_):
                rx = h_ % 2
                row = (j_ * 8 + h_) * 64
                for half in range(2):
                    DMA("sp", XSp[rx][half * 64:(half + 1) * 64],
                        rpbx_d[row:row + 64, :].rearrange("p (a b) -> p a b", b=64), dXSp[rx], (), [bXSp[rx]])

            xs_load(0)
            for h_ in range(8):
                rx = h_ % 2
                u_ = j_ * 8 + h_
                if h_ + 1 < 8:
                    xs_load(h_ + 1)
                ACT(TBp[rx][0:64, 0, DD - 7:DD + 8, :], XSp[rx][0:64], AF.Exp, [bXSp[rx]], [bTBp[rx]])
                ACT(TBp[rx][64:128, 0, DD - 6:DD + 9, :], XSp[rx][64:128], AF.Exp, [bXSp[rx]], [bTBp[rx]])
                ACT(TBp[rx][0:64, 1, DD - 3:DD + 5, :], XSp[rx][0:64, 4:12, :], AF.Exp, [bXSp[rx]], [bTBp[rx]])
                ACT(TBp[rx][64:128, 1, DD - 2:DD + 6, :], XSp[rx][64:128, 4:12, :], AF.Exp, [bXSp[rx]], [bTBp[rx]])
                DMA("act", tab_d[u_ * 128:(u_ + 1) * 128, :], TBp[rx].rearrange("p a b c -> p (a b c)"),
                    dTABo[rx], [bTBp[rx]], [bTABd[j_][h_]])

        build_tables(0)
        WM = [alloc([8, 512], BF16) for _ in range(2)]
        bWM = [Buf(), Buf()]
        dWM = [dsem(), dsem()]
        dma0, mms0, fin0 = mod_plan(0, WM, bWM, dWM)
        dma0(0)
        for pc in range(12):
            if pc + 1 < 12:
                dma0(pc + 1)
            mms0(pc)
        fin0()
        S.barrier()
        state["top"] = PERSIST_TOP

        def modv(i, m, kc, s):
            return MOD[:, i, m, kc, s:s + 1]

        def layer_vectors(i):
            nm = VEC[:, VOFF["nmix"] + i * 8:VOFF["nmix"] + (i + 1) * 8]
            nf = VEC[:, VOFF["nffn"] + i * 8:VOFF["nffn"] + (i + 1) * 8]
            for s in range(2):
                STT("dve", AV[:, 0, :, s], MOD[:, i, 1, :, s], 1.0, nm, ALU.add, ALU.mult,
                    [bMODL[i], bVEC], [bAV])
                STT("dve", AV[:, 1, :, s], MOD[:, i, 4, :, s], 1.0, nf, ALU.add, ALU.mult,
                    [bMODL[i], bVEC], [bAV])

        def norm(i, which, H, HB):
            SQ = [alloc([8, 512], BF16) for _ in range(2)]
            bSQ = [Buf(), Buf()]
            SD = [alloc([512], F32) for _ in range(2)]
            RS = [alloc([512], F32) for _ in range(2)]
            bSD = [Buf(), Buf()]
            bRS = [Buf(), Buf()]
            TMP = [alloc([512], F32) for _ in range(4)]
            bTMP = [Buf() for _ in range(4)]
            bm = 0 if which == 0 else 3
            k = 0
            for ti, (t0, n, s) in enumerate(TL):
                r = ti % 2
                for kc in range(8):
                    ACT(SQ[r][:, kc, :], X[:, kc, t0:t0 + n], AF.Square, [XB[kc][ti]], [bSQ[r]])
                pa, bpa = bank()
                for kc in range(8):
                    MM(pa, ONES1024, SQ[r][:, kc, :], kc == 0, kc == 7, [bSQ[r], bCM], [bpa])
                ACT(SD[r], pa, AF.Ln, [bpa, bEPS], [bSD[r]], bias=EPSV, scale=1.0)
                ACT(RS[r], SD[r], AF.Exp, [bSD[r]], [bRS[r]], scale=-0.5)
                for kc in range(8):
                    q = k % 4
                    k += 1
                    TT("dve", TMP[q], X[:, kc, t0:t0 + n], RS[r], ALU.mult, [XB[kc][ti], bRS[r]], [bTMP[q]])
                    ACT(H[:, kc, t0:t0 + n], TMP[q], AF.Identity, [bTMP[q], bAV, bMODL[i]], [HB[kc][ti]],
                        bias=modv(i, bm, kc, s), scale=AV[:, which, kc, s:s + 1])

        def resid_add(pso, bpso, i, gm, oc, ti):
            t0, n, s = TL[ti]
            STT("dve", X[:, oc, t0:t0 + n], pso, modv(i, gm, oc, s), X[:, oc, t0:t0 + n],
                ALU.mult, ALU.add, [bpso, bMODL[i]], [XB[oc][ti]])

        EPSV = alloc([1], F32)
        bEPS = Buf()
        MEMSET("dve", EPSV, EPS, [bEPS])
        PERSIST_TOP = state["top"]
        S.barrier()

        def ffn(i, H, HB):
            WI = [alloc([8, 512], BF16) for _ in range(2)]
            WO = [alloc([2, 1024], BF16) for _ in range(4)]
            bWI = [Buf(), Buf()]
            bWO = [Buf() for _ in range(4)]
            dWI = [dsem(), dsem()]
            dWO = [dsem() for _ in range(4)]
            ACTB = [alloc([2, NT], BF16) for _ in range(3)]
            bACT = [[[Buf() for _ in range(5)] for _ in range(2)] for _ in range(3)]
            NY = 5
            YB = [alloc([512], F32) for _ in range(NY)]
            bY = [Buf() for _ in range(NY)]
            yk = {"k": 0}

            def ybuf():
                q = yk["k"] % NY
                yk["k"] += 1
                return YB[q], bY[q]

            def load(pc):
                row = (i * NPIECE + pc) * 128
                DMA("pool", WI[pc % 2], fwi_d[row:row + 128, :].rearrange("p (a b) -> p a b", b=512),
                    dWI[pc % 2], (), [bWI[pc % 2]])
                DMA("pool", WO[pc % 4], fwo_d[row:row + 128, :].rearrange("p (a b) -> p a b", b=1024),
                    dWO[pc % 4], (), [bWO[pc % 4]])

            def up(pc):
                r = pc % 2
                ra = pc % 3
                for f in range(2):
                    fc = pc * 2 + f
                    prev = [None, None]
                    ys = {}

                    def finish(ti):
                        t0, n, s = TL[ti]
                        ya, bya = ys.pop((0, ti))
                        yg, byg = ys.pop((1, ti))
                        ACT(ya, ya, AF.Silu, [bya], [bya])
                        TT("pool", ACTB[ra][:, f, t0:t0 + n], ya, yg, ALU.mult, [bya, byg], [bACT[ra][f][ti]])

                    for ti, (t0, n, s) in enumerate(TL):
                        for br in range(2):
                            ch = fc + 22 * br
                            w0, w1, w2 = (vec("fcw", i, ch, 0), vec("fcw", i, ch, 1), vec("fcw", i, ch, 2))
                            bb = vec("fcb", i, ch)
                            col = (f * 2 + br) * 128
                            pu, bpu = bank()
                            for kc in range(8):
                                MM(pu, WI[r][:, kc, col:col + 128], H[:, kc, t0:t0 + n], kc == 0, kc == 7,
                                   [bWI[r], HB[kc][ti]], [bpu])
                            y, by = ybuf()
                            ys[(br, ti)] = (y, by)
                            ACT(y, pu, AF.Identity, [bpu, bVEC], [by], bias=bb, scale=w1)
                            if s == 0:
                                rngs = [(0, 512)]
                            else:
                                rngs = [(0, 256), (256, 512)]
                            for (a, b) in rngs:
                                STT("dve", y[:, a + 1:b], pu[:, a:b - 1], w0, y[:, a + 1:b], ALU.mult, ALU.add,
                                    [bpu, by, bVEC], [by])
                                STT("dve", y[:, a:b - 1], pu[:, a + 1:b], w2, y[:, a:b - 1], ALU.mult, ALU.add,
                                    [bpu, by, bVEC], [by])
                            if s == 0 and ti > 0:
                                ppu, bppu, py, bpy = prev[br]
                                STT("dve", y[:, 0:1], ppu[:, 511:512], w0, y[:, 0:1], ALU.mult, ALU.add,
                                    [bppu, by, bVEC], [by])
                                STT("dve", py[:, 511:512], pu[:, 0:1], w2, py[:, 511:512], ALU.mult, ALU.add,
                                    [bpu, bpy, bVEC], [bpy])
                            prev[br] = (pu, bpu, y, by)
                        if s == 0 and ti > 0:
                            finish(ti - 1)
                        if (s == 0 and ti == 3) or s == 1:
                            finish(ti)

            def down(pcs):
                nmm = 2 * len(pcs)
                for ti, (t0, n, s) in enumerate(TL):
                    for oc in range(8):
                        po, bpo = bank()
                        k_ = 0
                        for pc in pcs:
                            for f in range(2):
                                MM(po, WO[pc % 4][:, f, oc * 128:(oc + 1) * 128], ACTB[pc % 3][:, f, t0:t0 + n],
                                   k_ == 0, k_ == nmm - 1, [bWO[pc % 4], bACT[pc % 3][f][ti]], [bpo])
                                k_ += 1
                        resid_add(po, bpo, i, 5, oc, ti)

            if i + 1 < 4:
                WMf = [alloc([8, 512], BF16)]
                mdma, mmms, mfin = mod_plan(i + 1, WMf, [Buf()], [dsem()])
                mdma(0)
            load(0)
            for pc in range(NPIECE):
                if pc + 1 < NPIECE:
                    load(pc + 1)
                if i + 1 < 4:
                    mmms(pc)
                    mdma(pc + 1)
                    if pc == NPIECE - 1:
                        mmms(pc + 1)
                up(pc)
                if pc >= 2 and pc % 2 == 0:
                    down([pc - 2, pc - 1])
            down([NPIECE - 1])
            if i + 1 < 4:
                mfin()

        LG = 2650
        GB = [15, 2093, 2379]

        def conformer(i, H, HB):
            j = i // 2
            top0 = state["top"]
            GLU = alloc([8, LG], BF16)
            bGLU = [Buf() for _ in range(8)]
            for c in range(8):
                MEMSET("dve", GLU[:, c, :], 0.0, [bGLU[c]])
            W1 = [alloc([8, 256], BF16) for _ in range(2)]
            bW1 = [Buf(), Buf()]
            dW1 = [dsem(), dsem()]
            SG = [alloc([512], F32) for _ in range(3)]
            bSG = [Buf() for _ in range(3)]
            k = 0
            for oc in range(8):
                r = oc % 2
                row = (j * 8 + oc) * 128
                DMA("pool", W1[r], pw1_d[row:row + 128, :].rearrange("p (a b) -> p a b", b=256),
                    dW1[r], (), [bW1[r]])
                for ti, (t0, n, s) in enumerate(TL):
                    pa, bpa = bank()
                    pg, bpg = bank()
                    for kc in range(8):
                        MM(pa, W1[r][:, kc, 0:128], H[:, kc, t0:t0 + n], kc == 0, kc == 7, [bW1[r], HB[kc][ti]], [bpa])
                    for kc in range(8):
                        MM(pg, W1[r][:, kc, 128:256], H[:, kc, t0:t0 + n], kc == 0, kc == 7, [bW1[r], HB[kc][ti]], [bpg])
                    q = k % 3
                    k += 1
                    ACT(SG[q], pg, AF.Sigmoid, [bpg], [bSG[q]])
                    if s == 0:
                        TT("dve", GLU[:, oc, GB[0] + t0:GB[0] + t0 + n], pa, SG[q], ALU.mult, [bpa, bSG[q]], [bGLU[oc]])
                    else:
                        for c2 in range(2):
                            TT("dve", GLU[:, oc, GB[1 + c2]:GB[1 + c2] + 256], pa[:, c2 * 256:(c2 + 1) * 256],
                               SG[q][:, c2 * 256:(c2 + 1) * 256], ALU.mult, [bpa, bSG[q]], [bGLU[oc]])
            S.barrier()
            state["top"] = top0
            GLU2 = alloc([8, LG], BF16)
            VB = H
            bVB = [[Buf() for _ in range(5)] for _ in range(8)]
            DG = [alloc([31, 128], BF16) for _ in range(2)]
            bDG = [Buf(), Buf()]
            dDG = [dsem(), dsem()]
            if i == 1:
                build_tables(1)
            for c in range(8):
                r = c % 2
                row = (j * 8 + c) * 128
                DMA("pool", DG[r], dwd_d[row:row + 128, :].rearrange("p (a b) -> p a b", b=128),
                    dDG[r], (), [bDG[r]])
                for ti, (t0, n, s) in enumerate(TL):
                    pv, bpv = bank()
                    if s == 0:
                        for jj in range(31):
                            st = GB[0] + t0 + jj - 15
                            MM(pv, DG[r][:, jj, :], GLU2[:, c, st:st + 512], jj == 0, jj == 30, [bDG[r], bGLU[c]], [bpv])
                    else:
                        for c2 in range(2):
                            for jj in range(31):
                                st = GB[1 + c2] + jj - 15
                                MM(pv[:, c2 * 256:(c2 + 1) * 256], DG[r][:, jj, :], GLU2[:, c, st:st + 256],
                                   jj == 0, jj == 30, [bDG[r], bGLU[c]], [bpv])
                    ACT(VB[:, c, t0:t0 + n], pv, AF.Identity, [bpv, bVEC], [bVB[c][ti]], bias=vec("dwb", j, c), scale=1.0)
            S.barrier()
            state["top"] = top0
            W2 = alloc([8, 1024], BF16)
            bW2 = Buf()
            dW2 = dsem()
            DMA("pool", W2, pw2_d[j * 128:(j + 1) * 128, :].rearrange("p (a b) -> p a b", b=1024), dW2, (), [bW2])
            SQ = [alloc([8, 512], BF16) for _ in range(2)]
            bSQ = [Buf(), Buf()]
            SS = [alloc([8, 512], BF16) for _ in range(2)]
            bSS = [[Buf() for _ in range(8)] for _ in range(2)]
            MS = [alloc([512], F32) for _ in range(2)]
            bMS = [Buf(), Buf()]
            M2 = [alloc([512], F32) for _ in range(2)]
            bM2 = [Buf(), Buf()]
            RS = [alloc([512], F32) for _ in range(2)]
            bRS = [Buf(), Buf()]
            T1 = [alloc([512], F32) for _ in range(4)]
            bT1 = [Buf() for _ in range(4)]
            k = 0
            for ti, (t0, n, s) in enumerate(TL):
                r = ti % 2
                for c in range(8):
                    ACT(SQ[r][:, c, :], VB[:, c, t0:t0 + n], AF.Square, [bVB[c][ti]], [bSQ[r]])
                pm, bpm = bank()
                pq, bpq = bank()
                for c in range(8):
                    MM(pm, ONES1024, VB[:, c, t0:t0 + n], c == 0, c == 7, [bVB[c][ti], bCM], [bpm])
                for c in range(8):
                    MM(pq, ONES1024, SQ[r][:, c, :], c == 0, c == 7, [bSQ[r], bCM], [bpq])
                ACT(MS[r], pm, AF.Identity, [bpm], [bMS[r]])
                TT("dve", M2[r], MS[r], MS[r], ALU.mult, [bMS[r]], [bM2[r]])
                TT("dve", M2[r], pq, M2[r], ALU.subtract, [bpq, bM2[r]], [bM2[r]])
                ACT(M2[r], M2[r], AF.Ln, [bM2[r], bEPS], [bM2[r]], bias=EPSV, scale=1.0)
                ACT(RS[r], M2[r], AF.Exp, [bM2[r]], [bRS[r]], scale=-0.5)
                for c in range(8):
                    q = k % 4
                    k += 1
                    TT("dve", T1[q], VB[:, c, t0:t0 + n], MS[r], ALU.subtract, [bVB[c][ti], bMS[r]], [bT1[q]])
                    TT("dve", T1[q], T1[q], RS[r], ALU.mult, [bT1[q], bRS[r]], [bT1[q]])
                    ACT(SS[r][:, c, :], T1[q], AF.Silu, [bT1[q], bVEC], [bSS[r][c]],
                        bias=vec("lnb", j, c), scale=vec("lng", j, c))
                for oc in range(8):
                    po, bpo = bank()
                    for c in range(8):
                        MM(po, W2[:, c, oc * 128:(oc + 1) * 128], SS[r][:, c, :], c == 0, c == 7, [bW2, bSS[r][c]], [bpo])
                    resid_add(po, bpo, i, 2, oc, ti)
            S.barrier()
            state["top"] = top0

        def even_mixer(i, H, HB):
            j = i // 2
            top0 = state["top"]
            WP = alloc([8, 512], BF16)
            bWP = Buf()
            dWP = dsem()
            for pc in range(2):
                row = (j * 8 + 6 + pc) * 128
                DMA("pool", WP[:, :, pc * 256:(pc + 1) * 256],
                    winab_d[row:row + 128, :].rearrange("p (a b) -> p a b", b=256), dWP, (), [bWP])
            PW = alloc([4, 128], BF16)
            WOP = alloc([4, 1024], BF16)
            bPW, bWOP = Buf(), Buf()
            DMA("pool", PW, poolw_d[j * 128:(j + 1) * 128, :].rearrange("p (a b) -> p a b", b=128), dsem(), (), [bPW])
            DMA("pool", WOP, wop_d[j * 128:(j + 1) * 128, :].rearrange("p (a b) -> p a b", b=1024), dsem(), (), [bWOP])
            PTM = alloc([20, 512], BF16)
            bPTM = [Buf() for _ in range(20)]
            for tt in range(20):
                ti = tt // 4
                pp, bpp = bank()
                for kc in range(8):
                    MM(pp, H[:, kc, tt * 128:(tt + 1) * 128], WP[:, kc, :], kc == 0, kc == 7, [HB[kc][ti], bWP], [bpp])
                ACT(PTM[:, tt, :], pp, AF.Identity, [bpp], [bPTM[tt]])
            BND = [alloc([6, 512], BF16) for _ in range(3)]
            bBND = [Buf(), Buf(), Buf()]
            dBND = [dsem(), dsem(), dsem()]
            DT = [alloc([512], BF16) for _ in range(2)]
            bDT = [Buf(), Buf()]
            YP = [alloc([4, 512], BF16) for _ in range(2)]
            bYP = [[Buf() for _ in range(4)] for _ in range(2)]
            k = 0
            for ti, (t0, n, s) in enumerate(TL):
                ry = ti % 2
                for g in range(4):
                    r = k % 3
                    k += 1
                    pd, bpd = bank()
                    if s == 0:
                        row = (g * 4 + ti) * 128
                        DMA("pool", BND[r], band_d[row:row + 128, :].rearrange("p (a b) -> p a b", b=512),
                            dBND[r], (), [bBND[r]])
                        its = [it for it in range(4 * ti - 1, 4 * ti + 5) if 0 <= it < 16]
                        for n_, it in enumerate(its):
                            MM(pd, PTM[:, it, g * 128:(g + 1) * 128], BND[r][:, it - (4 * ti - 1), :],
                               n_ == 0, n_ == len(its) - 1, [bPTM[it], bBND[r]], [bpd])
                    else:
                        DMA("pool", BND[r][:, 0:2, 0:256],
                            bandc_d[g * 128:(g + 1) * 128, :].rearrange("p (a b) -> p a b", b=256),
                            dBND[r], (), [bBND[r]])
                        for c2 in range(2):
                            for it2 in range(2):
                                it = 16 + 2 * c2 + it2
                                MM(pd[:, c2 * 256:(c2 + 1) * 256], PTM[:, it, g * 128:(g + 1) * 128],
                                   BND[r][:, it2, 0:256], it2 == 0, it2 == 1, [bPTM[it], bBND[r]], [bpd])
                    rd = k % 2
                    ACT(DT[rd], pd, AF.Identity, [bpd], [bDT[rd]])
                    py, bpy = bank()
                    MM(py, PW[:, g, :], DT[rd], True, True, [bPW, bDT[rd]], [bpy])
                    ACT(YP[ry][:, g, :], py, AF.Identity, [bpy, bVEC], [bYP[ry][g]], scale=vec("pscale", j, g))
                for oc in range(8):
                    po, bpo = bank()
                    for g in range(4):
                        MM(po, WOP[:, g, oc * 128:(oc + 1) * 128], YP[ry][:, g, :], g == 0, g == 3,
                           [bWOP, bYP[ry][g]], [bpo])
                    resid_add(po, bpo, i, 2, oc, ti)
            S.barrier()
            state["top"] = top0
            if stop_after == "pool":
                return
            h_lo = top0 - 8 * NT * 2
            KT = alloc([4, NT], BF16)
            bKT = [[Buf() for _ in range(5)] for _ in range(4)]
            VT = alloc([24, 8, 66], BF16)
            bVT = [Buf() for _ in range(24)]
            CK = alloc([4, 512], BF16)
            bCK = Buf()
            top_keep = state["top"]
            MEMSET("dve", VT[:, :, :, 64:66], 1.0, [bVT[tt] for tt in range(24)])
            DMA("pool", CK, ck_d[j * 128:(j + 1) * 128, :].rearrange("p (a b) -> p a b", b=512), dsem(), (), [bCK])
            dcv = dsem()
            for t in range(4):
                DMA("pool", VT[:, 20 + t, :, 0:64],
                    cv_d[j * 128:(j + 1) * 128, t * 512:(t + 1) * 512].rearrange("p (a b) -> p a b", b=64),
                    dcv, (), [bVT[20 + t]])
            if stop_after == "qkvA":
                S.barrier()
                return
            WQ = [alloc([8, 256], BF16) for _ in range(2)]
            bWQ = [Buf(), Buf()]
            dWQ = [dsem(), dsem()]
            SQ1 = [alloc([512], BF16) for _ in range(2)]
            bSQ1 = [Buf(), Buf()]
            SD = [alloc([512], F32) for _ in range(2)]
            bSD = [Buf(), Buf()]
            RS = [alloc([512], F32) for _ in range(2)]
            bRS = [Buf(), Buf()]
            QST = [alloc([512], BF16) for _ in range(3)]
            bQST = [Buf() for _ in range(3)]
            KOUT = [alloc([512], F32) for _ in range(2)]
            bKOUT = [Buf(), Buf()]
            bQS = [[Buf() for _ in range(5)] for _ in range(4)]
            dQS = dsem()
            k = 0
            kq = 0
            ko = 0
            for pc in range(4):
                r = pc % 2
                row = (j * 8 + pc) * 128
                DMA("pool", WQ[r], winab_d[row:row + 128, :].rearrange("p (a b) -> p a b", b=256),
                    dWQ[r], (), [bWQ[r]])
                for ti, (t0, n, s) in enumerate(TL):
                    for c2 in range(2):
                        ch = (pc % 2) * 2 + c2
                        pq, bpq = bank()
                        for kc in range(8):
                            MM(pq, WQ[r][:, kc, c2 * 128:(c2 + 1) * 128], H[:, kc, t0:t0 + n], kc == 0, kc == 7,
                               [bWQ[r], HB[kc][ti]], [bpq])
                        q = k % 2
                        k += 1
                        ACT(SQ1[q], pq, AF.Square, [bpq], [bSQ1[q]])
                        pn, bpn = bank()
                        MM(pn, BLK64, SQ1[q], True, True, [bCM, bSQ1[q]], [bpn])
                        ACT(SD[q], pn, AF.Ln, [bpn, bEPS], [bSD[q]], bias=EPSV, scale=1.0)
                        ACT(RS[q], SD[q], AF.Exp, [bSD[q]], [bRS[q]], scale=-0.5)
                        if pc < 2:
                            qq = kq % 3
                            kq += 1
                            STT("dve", QST[qq], pq, QG8, RS[q], ALU.mult, ALU.mult, [bpq, bQG8, bRS[q]], [bQST[qq]])
                            DMA("sp", qs_d[:, ch * NT + t0:ch * NT + t0 + n], QST[qq], dQS, [bQST[qq]], [bQS[ch][ti]])
                        else:
                            kgv = VEC[:, VOFF["kg"] + j:VOFF["kg"] + j + 1]
                            STT("dve", KT[:, ch, t0:t0 + n], pq, kgv, RS[q], ALU.mult, ALU.mult,
                                [bpq, bVEC, bRS[q]], [bKT[ch][ti]])
                            if s == 1:
                                o_ = ko % 2
                                ko += 1
                                STT("dve", KOUT[o_], pq, kgv, RS[q], ALU.mult, ALU.mult,
                                    [bpq, bVEC, bRS[q]], [bKOUT[o_]])
                                DMA("sp", nk_d[j * 128:(j + 1) * 128, ch * 512:(ch + 1) * 512], KOUT[o_], dout,
                                    [bKOUT[o_]], [])
            if stop_after == "qkvB":
                S.barrier()
                return
            for pc in range(2):
                row = (j * 8 + 4 + pc) * 128
                DMA("pool", WQ[pc], winab_d[row:row + 128, :].rearrange("p (a b) -> p a b", b=256),
                    dWQ[pc], (), [bWQ[pc]])
            VOUT = KOUT
            bVOUT = bKOUT
            for tt in range(20):
                ti = tt // 4
                pv, bpv = bank()
                for pc in range(2):
                    for kc in range(8):
                        MM(pv[:, pc * 256:(pc + 1) * 256], H[:, kc, tt * 128:(tt + 1) * 128], WQ[pc][:, kc, :],
                           kc == 0, kc == 7, [HB[kc][ti], bWQ[pc]], [bpv])
                if stop_after != "qkvC1":
                    for hh in range(8):
                        ACT(VT[:, tt, hh, 0:64], pv[:, hh * 64:(hh + 1) * 64], AF.Identity, [bpv], [bVT[tt]])
                if tt >= 16:
                    o_ = tt % 2
                    S.op("dve", (lambda o_=o_, pv=pv: (lambda e: e.tensor_copy(out=VOUT[o_], in_=pv)))(),
                         [bpv], [bVOUT[o_]])
                    DMA("sp", nv_d[j * 128:(j + 1) * 128, (tt - 16) * 512:(tt - 15) * 512], VOUT[o_], dout,
                        [bVOUT[o_]], [])
            S.barrier()
            if stop_after in ("qkv", "qkvC1"):
                return
            state["top"] = h_lo
            state["skip"] = (top0, top_keep)
            WOA = alloc([8, 1024], BF16)
            bWOA = Buf()
            MEMSET("dve", WOA[64:128], 0.0, [bWOA])
            DMA("pool", WOA[0:64], woa_d[j * 64:(j + 1) * 64, :].rearrange("p (a b) -> p a b", b=1024),
                dsem(), (), [bWOA])
            AT0 = alloc([8, 512], BF16)
            bAT0 = [Buf() for _ in range(8)]
            MEMSET("dve", AT0, 0.0, bAT0)
            AT = [AT0, AT0]
            bAT = [bAT0, bAT0]
            QZ = [alloc([8, 512], BF16) for _ in range(2)]
            bQT = [Buf(), Buf()]
            dQT = [dsem(), dsem()]
            for r_ in range(2):
                MEMSET("dve", QZ[r_], 0.0, [bQT[r_]])
            TBL = [alloc([2, NBLK, 64], BF16) for _ in range(2)]
            bTB = [Buf(), Buf()]
            dTB = [dsem(), dsem()]
            NE = 8
            EB = [alloc([512], BF16) for _ in range(NE)]
            bEB = [Buf() for _ in range(NE)]
            OS = [alloc([512], F32) for _ in range(2)]
            bOS = [Buf(), Buf()]
            RC = [alloc([512], F32) for _ in range(2)]
            bRC = [Buf(), Buf()]
            ek = {"k": 0, "hk": 0, "mk": 0}

            def ebuf():
                q = ek["k"] % NE
                ek["k"] += 1
                return EB[q], bEB[q]

            def meng():
                ek["mk"] += 1
                return "dve"

            def finalize(po, bpo, ncol, at, bat, h, c0):
                q = ek["hk"] % 2
                ek["hk"] += 1
                ACT(OS[q][0:65, 0:ncol], po[0:65, 0:ncol], AF.Identity, [bpo], [bOS[q]])
                RECIP(RC[q][64:65, 0:ncol], OS[q][64:65, 0:ncol], [bOS[q]], [bRC[q]])
                pb, bpb = bank()
                MM(pb[0:64, 0:ncol], ONESF[64:65, 0:64], RC[q][64:65, 0:ncol], True, True, [bONESF, bRC[q]], [bpb])
                TT("dve", at[0:64, h, c0:c0 + ncol], OS[q][0:64, 0:ncol], pb[0:64, 0:ncol], ALU.mult,
                   [bOS[q], bpb], [bat[h]])

            from collections import deque
            LOOK = 5
            DEFER = 5
            units = []
            for ti, (t0, n, s) in enumerate(TL):
                for h in range(8):
                    if s == 0:
                        units.append((ti, h, None))
                    else:
                        units.append((ti, h, "c"))
            steps = []
            for ui, (ti, h, c2) in enumerate(units):
                if c2 is None:
                    r0 = 8 * ti
                    rs0 = min(max(r0 - 4, 0), 24)
                    rs7 = min(max(r0 + 7 - 4, 0), 24)
                    tl = [("cache", t) for t in range(4)] + [("local", kr0) for kr0 in range(rs0, rs7 + 8, 2)]
                else:
                    tl = [("ctx", (cq, t)) for cq in range(2) for t in range(2)]
                for k_, d_ in enumerate(tl):
                    steps.append((ui, d_, k_ == 0, k_ == len(tl) - 1))
            local_units = [ui for ui, u in enumerate(units) if u[2] is None]
            recs = {}
            ust = {}
            pend = deque()
            reserved.clear()

            def load_q(ti):
                t0, n, s = TL[ti]
                rq = ti % 2
                qv = QZ[rq].rearrange("p (c two) n -> p c two n", two=2)
                for half in range(2):
                    src = qs_d[half * 64:(half + 1) * 64, :].rearrange("p (c t) -> p c t", t=NT)
                    DMA("sp", qv[half * 64:(half + 1) * 64, :, half, :], src[:, :, t0:t0 + n], dQT[rq],
                        [bQS[ch][ti] for ch in range(4)], [bQT[rq]])

            def load_tab(ui):
                ti, h, c2 = units[ui]
                rx = local_units.index(ui) % 2
                DMA("sp", TBL[rx].rearrange("p a b c -> p (a b c)"),
                    tab_d[(j * 8 + h) * 128:(j * 8 + h + 1) * 128, :], dTB[rx], [bTABd[j][h]], [bTB[rx]])

            utab = {}

            def unit_start(ui):
                ti, h, c2 = units[ui]
                if h == 0:
                    if ti == 0:
                        load_q(0)
                    if ti + 1 < len(TL):
                        load_q(ti + 1)
                if c2 is None:
                    li = local_units.index(ui)
                    if li == 0:
                        load_tab(ui)
                    if li + 1 < len(local_units):
                        load_tab(local_units[li + 1])
                    utab[ui] = li % 2

            def emit_S(idx):
                ui, (kind, arg), first, last = steps[idx]
                ti, h, c2 = units[ui]
                rq = ti % 2
                c = h // 2
                p0 = (h % 2) * 64
                psS, bps = bank()
                eb, beb = ebuf()
                if kind == "ctx":
                    cq, t = arg
                    tt = 16 + 2 * cq + t
                    MM(psS[:, 0:256], KT[:, c, tt * 128:(tt + 1) * 128],
                       QZ[rq][:, h, cq * 256:(cq + 1) * 256], True, True, [bKT[c][4], bQT[rq]], [bps])
                    ACT(eb[:, 0:256], psS[:, 0:256], AF.Exp, [bps], [beb])
                    recs[idx] = (eb[:, 0:256], beb, tt, 256, cq * 256, t == 0, t == 1)
                    return
                if kind == "cache":
                    t = arg
                    MM(psS, CK[:, c, t * 128:(t + 1) * 128], QZ[rq][:, h, :], True, True,
                       [bCK, bQT[rq]], [bps])
                    ACT(eb, psS, AF.Exp, [bps], [beb])
                    recs[idx] = (eb, beb, 20 + t, 512, 0, None, None)
                    return
                kr0 = arg
                r0 = 8 * ti
                rt = utab[ui]
                TE_, TI_, btb = TBL[rt][:, 0], TBL[rt][:, 1], bTB[rt]
                MM(psS, KT[:, c, kr0 * 64:kr0 * 64 + 128], QZ[rq][:, h, :], True, True,
                   [bKT[c][kr0 // 8], bQT[rq]], [bps])
                ACT(eb, psS, AF.Exp, [bps], [beb])

                def tslice(tab, qa, nrow):
                    j0 = DD - (kr0 - qa)
                    return tab[:, j0:j0 + nrow, :]

                def ev(a_, b_):
                    return eb[:, a_ * 64:b_ * 64].rearrange("p (a b) -> p a b", b=64)
                if ti in (1, 2):
                    TT("dve", ev(0, 8), ev(0, 8), tslice(TI_, r0, 8), ALU.mult, [beb, btb], [beb])
                elif ti == 0:
                    TT("dve", ev(4, 8), ev(4, 8), tslice(TI_, 4, 4), ALU.mult, [beb, btb], [beb])
                    if kr0 < 8:
                        TT("dve", ev(0, 4), ev(0, 4), tslice(TE_, 0, 4), ALU.mult, [beb, btb], [beb])
                    else:
                        MEMSET("dve", eb[:, 0:256], 0.0, [beb])
                else:
                    TT("dve", ev(0, 5), ev(0, 5), tslice(TI_, 24, 5), ALU.mult, [beb, btb], [beb])
                    if kr0 >= 24:
                        TT("dve", ev(5, 8), ev(5, 8), tslice(TE_, 29, 3), ALU.mult, [beb, btb], [beb])
                    else:
                        MEMSET("dve", eb[:, 320:512], 0.0, [beb])
                recs[idx] = (eb, beb, kr0 // 2, 512, 0, None, None)

            def out_proj(ti):
                rq = ti % 2
                at, bat = AT[rq], bAT[rq]
                for oc in range(8):
                    po2, bpo2 = bank()
                    for h in range(8):
                        MM(po2, WOA[:, h, oc * 128:(oc + 1) * 128], at[:, h, :], h == 0, h == 7,
                           [bWOA, bat[h]], [bpo2])
                    resid_add(po2, bpo2, i, 2, oc, ti)

            def emit_PV(idx, now):
                ui, (kind, arg), first, last = steps[idx]
                ti, h, c2 = units[ui]
                eb, beb, tt, ncol, cofs, st_, sp_ = recs.pop(idx)
                st_ = first if st_ is None else st_
                sp_ = last if sp_ is None else sp_
                if first:
                    i_ = state["bank"]
                    po, bpo = bank(reserve=True)
                    ust[ui] = (po, bpo, (state["bank"] - 1) % 8)
                po, bpo, bi = ust[ui]
                vflat = VT[:, tt, :, :].rearrange("p a b -> p (a b)")
                mw = min(128, 8 * 66 - h * 66)
                MM(po[0:mw, cofs:cofs + ncol], vflat[:, h * 66:h * 66 + mw], eb, st_, sp_, [bVT[tt], beb], [bpo])
                if last:
                    q = ek["hk"] % 2
                    ek["hk"] += 1
                    rq = ti % 2
                    at, bat = AT[rq], bAT[rq]
                    c0 = 0
                    ncol = 512
                    ACT(OS[q][0:65, 0:ncol], po[0:65, 0:ncol], AF.Identity, [bpo], [bOS[q]])
                    ACT(RC[q][64:65, 0:ncol], OS[q][64:65, 0:ncol], AF.Ln, [bOS[q]], [bRC[q]])
                    ACT(RC[q][64:65, 0:ncol], RC[q][64:65, 0:ncol], AF.Exp, [bRC[q]], [bRC[q]], scale=-1.0)

                    def stage_b(q=q, ncol=ncol, at=at, bat=bat, h=h, c0=c0, bi=bi, ti=ti, c2=c2, ui=ui):
                        pb, bpb = bank()
                        MM(pb[0:64, 0:ncol], ONESF[64:65, 0:64], RC[q][64:65, 0:ncol], True, True,
                           [bONESF, bRC[q]], [bpb])
                        TT("dve", at[0:64, h, c0:c0 + ncol], OS[q][0:64, 0:ncol], pb[0:64, 0:ncol], ALU.mult,
                           [bOS[q], bpb], [bat[h]])
                        reserved.discard(bi)
                        del ust[ui]
                        if h == 7:
                            out_proj(ti)
                    pend.append((now + (DEFER if c2 is None else 3), stage_b))

            nst = len(steps)
            for idx in range(nst + LOOK + DEFER + 2):
                while pend and pend[0][0] <= idx:
                    pend.popleft()[1]()
                if idx < nst:
                    if steps[idx][2]:
                        unit_start(steps[idx][0])
                    emit_S(idx)
                jn = idx - LOOK
                if 0 <= jn < nst and steps[jn][2] and units[steps[jn][0]][2] == "c" and units[steps[jn][0]][1] == 0:
                    while pend:
                        pend.popleft()[1]()
                jdx = idx - LOOK
                if 0 <= jdx < nst:
                    emit_PV(jdx, idx)
            assert not pend and not ust and not recs
            reserved.clear()
            S.barrier()
            state["skip"] = None
            state["top"] = top0

        done = False
        for i in range(4):
            if done or stop_after == "mod":
                break
            layer_vectors(i)
            if i % 2 == 0:
                j = i // 2
                S.op("act", (lambda j=j: (lambda e: e.mul(QG8, VEC[:, VOFF["qg"] + j:VOFF["qg"] + j + 1], 0.125)))(),
                     [bVEC], [bQG8])
            top = state["top"]
            H = alloc([8, NT], BF16)
            HB = [[Buf() for _ in range(5)] for _ in range(8)]
            toph = state["top"]
            norm(i, 0, H, HB)
            S.barrier()
            state["top"] = toph
            if stop_after == "norm":
                break
            if i % 2 == 0:
                even_mixer(i, H, HB)
            else:
                conformer(i, H, HB)
            S.barrier()
            state["top"] = top
            if stop_after == (i, 0) or stop_after in ("pool", "qkv", "qkvA", "qkvB", "qkvC1"):
                break
            H = alloc([8, NT], BF16)
            HB = [[Buf() for _ in range(5)] for _ in range(8)]
            toph = state["top"]
            norm(i, 1, H, HB)
            S.barrier()
            state["top"] = toph
            ffn(i, H, HB)
            S.barrier()
            state["top"] = top
            if stop_after == (i, 1):
                break

        for kc in range(8):
            DMA("sp", yT_d[:, kc * NT:(kc + 1) * NT], X[:, kc, :], dout, [XB[kc][t] for t in range(5)], [])
        S.emit(block, sems, final_waits=[dout])
    return nc


def _fm(v):
    v = np.asarray(v, np.float32)
    lead = v.shape[:-1]
    n = v.shape[-1] // 128
    v = v.reshape(lead + (n, 128))
    return np.moveaxis(v, -1, 0)


def _band_matrices():
    def dmat(S_, w):
        t = np.arange(S_)
        lo = np.clip(t - w // 2, 0, S_ - 1)
        hi = np.clip(t - w // 2 + w - 1, 0, S_ - 1)
        D = np.zeros((S_, S_), np.float64)
        for a in range(S_):
            D[a, lo[a]:hi[a] + 1] = 1.0 / (hi[a] - lo[a] + 1)
        D -= np.eye(S_)
        return D.T.astype(np.float32)
    band = np.zeros((4, 4, 128, 6, 512), np.float32)
    bandc = np.zeros((4, 128, 2, 256), np.float32)
    for g, w in enumerate((2, 4, 8, 16)):
        DT = dmat(2048, w)
        for ot in range(4):
            for rel in range(6):
                it = 4 * ot - 1 + rel
                if 0 <= it < 16:
                    band[g, ot, :, rel, :] = DT[it * 128:(it + 1) * 128, ot * 512:(ot + 1) * 512]
        DC = dmat(256, w)
        for it in range(2):
            bandc[g, :, it, :] = DC[it * 128:(it + 1) * 128, :]
    return band.reshape(4 * 4 * 128, 6 * 512), bandc.reshape(4 * 128, 512)


def _prep_shared(inp):
    f = lambda k: np.asarray(inp[k], np.float32)
    sh = {}
    w_mod = f("w_mod")
    sh["wmod"] = np.ascontiguousarray(
        w_mod.reshape(4, 8, 128, 12, 512).transpose(0, 3, 2, 1, 4)).reshape(4 * 12 * 128, 4096)
    w_in = f("w_in_ab")
    sh["winab"] = np.ascontiguousarray(
        w_in.reshape(2, 8, 128, 8, 256).transpose(0, 3, 2, 1, 4)).reshape(2 * 8 * 128, 2048)
    w_out = f("w_out_ab")
    sh["woa"] = np.ascontiguousarray(
        w_out[:, :512].reshape(2, 8, 64, 1024).transpose(0, 2, 1, 3)).reshape(2 * 64, 8192)
    sh["wop"] = np.ascontiguousarray(
        w_out[:, 512:].reshape(2, 4, 128, 1024).transpose(0, 2, 1, 3)).reshape(2 * 128, 4096)
    sh["poolw"] = np.ascontiguousarray(f("pool_w").transpose(0, 2, 1, 3)).reshape(2 * 128, 512)
    sh["band"], sh["bandc"] = _band_matrices()
    rpb = f("rpb")
    c = np.arange(64)
    cs = np.clip(c - 8, 0, 48)
    valid = (c[:, None] >= cs[None, :]) & (c[:, None] < cs[None, :] + 16)
    dcol = np.clip(c[:, None] - c[None, :], -15, 15) + 15
    drs = np.arange(7, -8, -1) + 7
    ex = rpb[:, :, drs][:, :, :, dcol]
    ex = np.where(valid[None, None, None], ex, np.float32(-1e30))
    sh["rpbx"] = np.ascontiguousarray(ex.transpose(0, 1, 3, 2, 4)).reshape(2 * 8 * 64, 960)
    pw1 = f("conv_pw1")
    a = pw1[:, :, :1024].reshape(2, 8, 128, 8, 128)
    g = pw1[:, :, 1024:].reshape(2, 8, 128, 8, 128)
    ag = np.stack([a, g], axis=4)
    sh["pw1"] = np.ascontiguousarray(ag.transpose(0, 3, 2, 1, 4, 5)).reshape(2 * 8 * 128, 2048)
    sh["pw2"] = np.ascontiguousarray(
        f("conv_pw2").reshape(2, 8, 128, 1024).transpose(0, 2, 1, 3)).reshape(2 * 128, 8192)
    dw = f("conv_dw")
    dwd = np.zeros((2, 8, 128, 31, 128), np.float32)
    idx = np.arange(128)
    dwc = dw.reshape(2, 31, 8, 128)
    for jj in range(31):
        dwd[:, :, idx, jj, idx] = dwc[:, jj]
    sh["dwd"] = dwd.reshape(2 * 8 * 128, 31 * 128)
    wi = f("ffn_w_in")
    wa = wi[:, :, :2816].reshape(4, 8, 128, 11, 2, 128)
    wg = wi[:, :, 2816:].reshape(4, 8, 128, 11, 2, 128)
    wag = np.stack([wa, wg], axis=5)
    sh["fwi"] = np.ascontiguousarray(wag.transpose(0, 3, 2, 1, 4, 5, 6)).reshape(4 * NPIECE * 128, 4096)
    wo = f("ffn_w_out")
    sh["fwo"] = np.ascontiguousarray(
        wo.reshape(4, 11, 2, 128, 1024).transpose(0, 1, 3, 2, 4)).reshape(4 * NPIECE * 128, 2048)
    cm = np.zeros((128, 4, 128), np.float32)
    cm[:, 0, :] = 1.0 / 1024.0
    cm[:64, 1, :64] = 1.0 / 64.0
    cm[64:, 1, 64:] = 1.0 / 64.0
    cm[:, 2, :] = 1.0
    cm[idx, 3, idx] = 1.0
    sh["cmat"] = cm.reshape(128, 512)
    vt = np.zeros((128, NV), np.float32)

    def put(name, arr):
        arr = np.asarray(arr, np.float32).reshape(128, -1)
        vt[:, VOFF[name]:VOFF[name] + arr.shape[1]] = arr
    put("bmod", _fm(f("b_mod")))
    put("nmix", _fm(f("norm_mix")))
    put("nffn", _fm(f("norm_ffn")))
    put("fcw", np.moveaxis(_fm(f("ffn_conv_w")), 2, 3))
    put("fcb", _fm(f("ffn_conv_b")))
    put("dwb", _fm(f("conv_dw_b")))
    put("lng", _fm(f("conv_ln_g")))
    put("lnb", _fm(f("conv_ln_b")))
    put("pscale", _fm(f("pool_scale")))
    put("qg", np.tile(f("q_gain"), (1, 2)).T)
    put("kg", np.tile(f("k_gain"), (1, 2)).T)
    sh["_vt"] = vt
    return sh


def _prep_core(inp, sh, b):
    f = lambda k: np.asarray(inp[k], np.float32)
    xs = f("x_sample")[b]
    xp = f("x_prompt")[2 * b:2 * b + 2].reshape(512, 1024)
    xa = np.concatenate([xs, xp], axis=0)
    xT = np.ascontiguousarray(xa.T.reshape(8, 128, NT).transpose(1, 0, 2)).reshape(128, 8 * NT)
    vt = sh["_vt"].copy()
    cv = np.stack([_fm(f("c")[b]), _fm(f("c_ctx"))], axis=-1)
    vt[:, VOFF["cvec"]:VOFF["cvec"] + 16] = cv.reshape(128, 16)
    ck = f("cache_k")[b]
    ckT = ck.reshape(2, 512, 4, 128).transpose(0, 3, 2, 1)
    cvv = f("cache_v")[b].reshape(2, 4, 128, 512).transpose(0, 2, 1, 3)
    m = {k: v for k, v in sh.items() if not k.startswith("_")}
    m["xT"] = xT
    m["vecs"] = vt
    m["ck"] = np.ascontiguousarray(ckT).reshape(2 * 128, 2048)
    m["cv"] = np.ascontiguousarray(cvv).reshape(2 * 128, 2048)
    return m


_NC_CACHE = {}


_NCORES = [8]


def kernel(**inputs):
    ncores = _NCORES[0]
    sh = _prep_shared(inputs)
    in_maps = [_prep_core(inputs, sh, b) for b in range(ncores)]
    if "nc" not in _NC_CACHE:
        _NC_CACHE["nc"] = build_program()
    nc = _NC_CACHE["nc"]
    res = run_bass_kernel_spmd(nc, in_maps, core_ids=list(range(ncores)))
    y_prompt = np.zeros((16, 256, 1024), np.float32)
    y_sample = np.zeros((8, 2048, 1024), np.float32)
    nk = np.zeros((16, 2, 256, 8, 64), np.float32)
    nv = np.zeros((16, 2, 256, 8, 64), np.float32)
    for b in range(ncores):
        r = res.results[b]
        yT = np.asarray(r["yT"], np.float32).reshape(128, 8, NT).transpose(2, 1, 0).reshape(NT, 1024)
        y_sample[b] = yT[:2048]
        y_prompt[2 * b:2 * b + 2] = yT[2048:].reshape(2, 256, 1024)
        k_ = np.asarray(r["nk"], np.float32).reshape(2, 128, 4, 2, 256)
        k_ = k_.transpose(3, 0, 4, 2, 1).reshape(2, 2, 256, 512)
        nk[2 * b:2 * b + 2] = k_.reshape(2, 2, 256, 8, 64)
        v_ = np.asarray(r["nv"], np.float32).reshape(2, 128, 4, 512)
        v_ = v_.transpose(0, 2, 1, 3).reshape(2, 2, 256, 512)
        nv[2 * b:2 * b + 2] = v_.transpose(1, 0, 2, 3).reshape(2, 2, 256, 8, 64)
    return (y_prompt, y_sample, nk, nv)
```
